# Optimizing a Trainium2 kernel written in Bass

```python
import jax, jax.numpy as jnp
from jax import lax
import numpy as np

D_MODEL = 1024
BATCH = 8
SEQ = 4096
DEPTH = 1

N_MEM = 256
CONV_WIDTH = D_MODEL // 2
CONV_KERNEL = 31
RET_WIDTH = D_MODEL - CONV_WIDTH
RET_HEADS = 4
RET_HEAD_DIM = RET_WIDTH // RET_HEADS
RET_CHUNK = 128
ROPE_BASE = 10000.0
IN_COLS = 2 * CONV_WIDTH + 4 * RET_WIDTH
SPLITS = (CONV_WIDTH, 2 * CONV_WIDTH, 2 * CONV_WIDTH + RET_WIDTH,
          2 * CONV_WIDTH + 2 * RET_WIDTH, 2 * CONV_WIDTH + 3 * RET_WIDTH)
XATTN_HEADS = 4
XATTN_HEAD_DIM = D_MODEL // XATTN_HEADS
D_FF = 4 * D_MODEL
LN_EPS = 1e-5
DN_ALPHA = (2.0 * DEPTH) ** 0.25
DN_BETA = (8.0 * DEPTH) ** -0.25

kernel_name = "deepnorm_conformer_retention_hybrid"


def layer_norm(x, g, b):
    xf = x.astype(jnp.float32)
    mu = jnp.mean(xf, axis=-1, keepdims=True)
    var = jnp.mean(jnp.square(xf - mu), axis=-1, keepdims=True)
    y = (xf - mu) * lax.rsqrt(var + LN_EPS)
    return (y * g.astype(jnp.float32) + b.astype(jnp.float32)).astype(x.dtype)


def rotary(t, pos):
    half = t.shape[-1] // 2
    inv = ROPE_BASE ** (-jnp.linspace(0.0, 1.0, half, dtype=jnp.float32))
    ang = pos[:, None] * inv[None, :]
    cos = jnp.cos(ang)[None, :, None, :].astype(t.dtype)
    sin = jnp.sin(ang)[None, :, None, :].astype(t.dtype)
    t1, t2 = t[..., :half], t[..., half:]
    return jnp.concatenate([t1 * cos - t2 * sin, t2 * cos + t1 * sin], axis=-1)


def retention_chunkwise(q, k, v):
    B, S, H, d = q.shape
    C = RET_CHUNK
    n_chunks = S // C
    log_g = jnp.log(1.0 - 2.0 ** (-5.0 - jnp.arange(H, dtype=jnp.float32)))
    idx = jnp.arange(C, dtype=jnp.float32)
    rel = idx[:, None] - idx[None, :]
    decay = jnp.where(rel >= 0, jnp.exp(log_g[:, None, None] * jnp.maximum(rel, 0.0)), 0.0)
    q_dec = jnp.exp(log_g[:, None] * (idx + 1.0))
    k_dec = jnp.exp(log_g[:, None] * (C - 1.0 - idx))
    chunk_dec = jnp.exp(log_g * C)

    def to_chunks(t):
        return t.reshape(B, n_chunks, C, H, d).transpose(1, 0, 3, 2, 4)

    def step(state, qkv):
        qc, kc, vc = qkv
        scores = jnp.einsum('bhnd,bhmd->bhnm', qc, kc) * decay[None]
        intra = jnp.einsum('bhnm,bhmd->bhnd', scores, vc)
        cross = jnp.einsum('bhnd,bhde->bhne', qc * q_dec[None, :, :, None], state)
        new_state = state * chunk_dec[None, :, None, None] + jnp.einsum(
            'bhmd,bhme->bhde', kc * k_dec[None, :, :, None], vc)
        return new_state, intra + cross

    state0 = jnp.zeros((B, H, d, d), jnp.float32)
    _, out = lax.scan(step, state0, (to_chunks(q), to_chunks(k), to_chunks(v)))
    return out.transpose(1, 0, 3, 2, 4).reshape(B, S, H, d)


def hybrid_mixer(x, w_in, conv_w, conv_b, conv_ln_g, conv_ln_b, ret_gn_g, ret_gn_b, w_out):
    B, S, _ = x.shape
    h = x @ w_in
    a, b, q, k, v, g = jnp.split(h, SPLITS, axis=-1)
    u = a * jax.nn.sigmoid(b)
    u = lax.conv_general_dilated(
        u, conv_w[:, None, :].astype(u.dtype), window_strides=(1,),
        padding=[(CONV_KERNEL - 1, 0)], dimension_numbers=('NWC', 'WIO', 'NWC'),
        feature_group_count=CONV_WIDTH) + conv_b
    conv_out = jax.nn.silu(layer_norm(u, conv_ln_g, conv_ln_b))
    pos = jnp.arange(S, dtype=jnp.float32)
    q = rotary(q.reshape(B, S, RET_HEADS, RET_HEAD_DIM), pos) * (RET_HEAD_DIM ** -0.5)
    k = rotary(k.reshape(B, S, RET_HEADS, RET_HEAD_DIM), pos)
    v = v.reshape(B, S, RET_HEADS, RET_HEAD_DIM)
    y = retention_chunkwise(q.astype(jnp.float32), k.astype(jnp.float32), v.astype(jnp.float32))
    mu = jnp.mean(y, axis=-1, keepdims=True)
    var = jnp.mean(jnp.square(y - mu), axis=-1, keepdims=True)
    y = ((y - mu) * lax.rsqrt(var + LN_EPS)).reshape(B, S, RET_WIDTH)
    y = y * ret_gn_g.astype(jnp.float32) + ret_gn_b.astype(jnp.float32)
    ret_out = jax.nn.silu(g) * y.astype(x.dtype)
    return jnp.concatenate([conv_out, ret_out], axis=-1) @ w_out


def memory_cross_attention(x, mem, w_xq, w_xk, w_xv, w_xo):
    B, S, D = x.shape
    M = mem.shape[1]
    q = (x @ w_xq).reshape(B, S, XATTN_HEADS, XATTN_HEAD_DIM)
    k = (mem @ w_xk).reshape(B, M, XATTN_HEADS, XATTN_HEAD_DIM)
    v = (mem @ w_xv).reshape(B, M, XATTN_HEADS, XATTN_HEAD_DIM)
    s = jnp.einsum('bshd,bmhd->bhsm', q, k).astype(jnp.float32) * (XATTN_HEAD_DIM ** -0.5)
    p = jax.nn.softmax(s, axis=-1).astype(x.dtype)
    o = jnp.einsum('bhsm,bmhd->bshd', p, v).reshape(B, S, D)
    return o @ w_xo


def squared_relu_mlp(x, w_up, w_down):
    return jnp.square(jax.nn.relu(x @ w_up)) @ w_down


def setup_inputs(seed: int = 0) -> dict:
    key = jax.random.key(seed)
    ks = jax.random.split(key, 22)
    f32 = jnp.float32
    L, D = DEPTH, D_MODEL

    def dense(k, shape, fan_in, scale=1.0):
        return jax.random.normal(k, shape, f32) * (scale * fan_in ** -0.5)

    def gain(k, shape):
        return 1.0 + 0.02 * jax.random.normal(k, shape, f32)

    def bias(k, shape):
        return 0.02 * jax.random.normal(k, shape, f32)

    return {
        'x': jax.random.normal(ks[0], (BATCH, SEQ, D), f32),
        'mem': jax.random.normal(ks[1], (BATCH, N_MEM, D), f32),
        'w_in': dense(ks[2], (L, D, IN_COLS), D),
        'conv_w': dense(ks[3], (L, CONV_KERNEL, CONV_WIDTH), CONV_KERNEL),
        'conv_b': bias(ks[4], (L, CONV_WIDTH)),
        'conv_ln_g': gain(ks[5], (L, CONV_WIDTH)),
        'conv_ln_b': bias(ks[6], (L, CONV_WIDTH)),
        'ret_gn_g': gain(ks[7], (L, RET_WIDTH)),
        'ret_gn_b': bias(ks[8], (L, RET_WIDTH)),
        'w_out': dense(ks[9], (L, D, D), D, DN_BETA),
        'ln1_g': gain(ks[10], (L, D)),
        'ln1_b': bias(ks[11], (L, D)),
        'w_xq': dense(ks[12], (L, D, D), D),
        'w_xk': dense(ks[13], (L, D, D), D),
        'w_xv': dense(ks[14], (L, D, D), D, DN_BETA),
        'w_xo': dense(ks[15], (L, D, D), D, DN_BETA),
        'ln2_g': gain(ks[16], (L, D)),
        'ln2_b': bias(ks[17], (L, D)),
        'w_up': dense(ks[18], (L, D, D_FF), D, DN_BETA),
        'w_down': dense(ks[19], (L, D_FF, D), D_FF, DN_BETA),
        'ln3_g': gain(ks[20], (L, D)),
        'ln3_b': bias(ks[21], (L, D)),
    }


def reference(x, mem, w_in, conv_w, conv_b, conv_ln_g, conv_ln_b, ret_gn_g, ret_gn_b, w_out,
              ln1_g, ln1_b, w_xq, w_xk, w_xv, w_xo, ln2_g, ln2_b, w_up, w_down, ln3_g, ln3_b):
    for l in range(DEPTH):
        mix = hybrid_mixer(x, w_in[l], conv_w[l], conv_b[l], conv_ln_g[l], conv_ln_b[l],
                           ret_gn_g[l], ret_gn_b[l], w_out[l])
        x = layer_norm(DN_ALPHA * x + mix, ln1_g[l], ln1_b[l])
        xa = memory_cross_attention(x, mem, w_xq[l], w_xk[l], w_xv[l], w_xo[l])
        x = layer_norm(DN_ALPHA * x + xa, ln2_g[l], ln2_b[l])
        ff = squared_relu_mlp(x, w_up[l], w_down[l])
        x = layer_norm(DN_ALPHA * x + ff, ln3_g[l], ln3_b[l])
    return x
```

```python
from contextlib import ExitStack

import numpy as np
import concourse.bass as bass
import concourse.mybir as mybir
from concourse.bass_utils import run_bass_kernel_spmd

F32 = mybir.dt.float32
BF16 = mybir.dt.bfloat16
AF = mybir.ActivationFunctionType
ALU = mybir.AluOpType

D = 1024
SEQ = 4096
NB = 8
NMEM = 256
T = 512
ALPHA = 2.0 ** 0.25
EPS = 1e-5
NV = 192
GAM = [1.0 - 2.0 ** (-5.0 - h) for h in range(4)]

ENGS = ("pe", "act", "dve", "pool", "sp")

ENG_SCALE = {}
NPSF = 7
PSB1K = 1 if NPSF == 6 else 0
LNMODE = 0


class _Op:
    __slots__ = ("eng", "fn", "signal", "waits", "dma", "sigval", "fin", "ph", "idx")

    def __init__(self, eng, fn, dma):
        self.eng = eng
        self.fn = fn
        self.dma = dma
        self.signal = False
        self.waits = []
        self.sigval = None
        self.fin = 0.0


class Prog:
    def __init__(self):
        self.ops = {e: [] for e in ENGS}
        self.last_w = {}
        self.readers = {}
        self.dma_counts = {}
        self.free = {e: 0.0 for e in ENGS}
        self.ph = {"f": "pro", "b": "pro"}
        self.tile = {"f": 0, "b": 0}
        self.spans = {}
        self.cur = "f"
        self.gaps = {}
        self.busy = {e: 0.0 for e in ENGS}
        self.mark = 0.0
        self.first = None

    def add(self, eng, fn, reads=(), writes=(), dma=None):
        op = _Op(eng, fn, dma)
        op.idx = len(self.ops[eng])
        if dma is not None:
            self.dma_counts[dma] = self.dma_counts.get(dma, 0) + 16
            op.sigval = self.dma_counts[dma]
        deps = []
        for k in reads:
            w = self.last_w.get(k)
            if w is not None:
                deps.append((w, k))
        for k in writes:
            w = self.last_w.get(k)
            if w is not None:
                deps.append((w, k))
            deps.extend((r, k) for r in self.readers.get(k, {}).values())
        t0 = self.free[eng]
        op.ph = self.ph[self.cur]
        crit = None
        for d, k in deps:
            lat = 0.0 if (d.eng == eng and d.dma is None) else 0.06
            if d.fin + lat > t0:
                t0 = d.fin + lat
                crit = d
        if crit is not None and t0 > self.free[eng]:
            gk = (eng, op.ph, crit.eng if crit.dma is None else "dma", crit.ph)
            self.gaps[gk] = self.gaps.get(gk, 0.0) + t0 - self.free[eng]
        best = {}
        for d, k in deps:
            if d is op:
                continue
            if d.dma is None and d.eng == eng:
                if eng == "pe":
                    continue
                if isinstance(k, tuple) and k[0] in ("psf", "psb"):
                    continue
            g = ("dma", d.dma) if d.dma is not None else ("eng", d.eng)
            cur = best.get(g)
            if cur is None or (d.sigval > cur.sigval if d.dma is not None else d.idx > cur.idx):
                best[g] = d
        for d in best.values():
            op.waits.append(d)
            d.signal = True
        g_self = ("dma", dma) if dma is not None else ("eng", eng)
        for k in reads:
            self.readers.setdefault(k, {})[g_self] = op
        for k in writes:
            self.last_w[k] = op
            self.readers[k] = {}
        dur = getattr(fn, "dur", 0.5)
        if eng == "pool" and dma is None:
            dur = 2.0 * dur
        if dma is None:
            dur *= ENG_SCALE.get(eng, 1.0)
        op.fin = t0 + dur
        self.busy[eng] += dur
        if self.cur == "b" and self.tile["b"] == 3 and op.ph == "wout" and getattr(self, "dump", False) and eng != "sp":
            print("    OP %-4s t0=%8.2f fin=%8.2f dur=%5.2f qwait=%6.2f crit=%s" % (
                eng, t0, op.fin, dur, t0 - self.free[eng] if False else 0.0,
                (crit.eng + "/" + crit.ph + "@%.1f" % crit.fin) if crit is not None else "-"))
        sk = (self.cur, self.tile[self.cur], op.ph)
        sp = self.spans.get(sk)
        if sp is None:
            self.spans[sk] = [t0, op.fin, 0.0]
            sp = self.spans[sk]
        sp[1] = max(sp[1], op.fin)
        if eng == "pe":
            sp[2] += dur
        self.free[eng] = t0 + (0.1 if dma is not None else dur)
        if op.fin > self.mark:
            self.mark = op.fin
        if self.first is None:
            self.first = op.fin
        self.ops[eng].append(op)
        return op

    def emit(self, block, sems, dma_sems, final_waits):
        for e in ENGS:
            c = 0
            for op in self.ops[e]:
                if op.dma is None and op.signal:
                    c += 1
                    op.sigval = c
            self.sigcount = getattr(self, "sigcount", {})
            self.sigcount[e] = (c, len(self.ops[e]))
        prog = self

        def run(eng_name):
            def body(eng):
                known = {}
                for op in prog.ops[eng_name]:
                    need = {}
                    for d in op.waits:
                        if d.dma is not None:
                            key = ("dma", d.dma)
                            sem = dma_sems[d.dma]
                        else:
                            key = ("eng", d.eng)
                            sem = sems[d.eng]
                        if need.get(key, (None, 0))[1] < d.sigval:
                            need[key] = (sem, d.sigval)
                    for key, (sem, v) in need.items():
                        if known.get(key, 0) >= v:
                            continue
                        known[key] = v
                        eng.wait_ge(sem, v)
                    inst = op.fn(eng)
                    if op.dma is not None:
                        inst.then_inc(dma_sems[op.dma], 16)
                    elif op.signal:
                        inst.then_inc(sems[eng_name], 1)
                for key in final_waits.get(eng_name, ()):
                    eng.wait_ge(dma_sems[key], prog.dma_counts[key])
            return body

        block.tensor(run("pe"))
        block.scalar(run("act"))
        block.vector(run("dve"))
        block.gpsimd(run("pool"))
        block.sync(run("sp"))


def _fsz(ap):
    n = 1
    for d in ap.shape[1:]:
        n *= int(d)
    return n


def _with(fn, dur):
    fn.dur = dur
    return fn


def f_mm(out, lhsT, rhs, start, stop):
    return _with(lambda e: e.matmul(out, lhsT=lhsT, rhs=rhs, start=start, stop=stop), 0.02 + _fsz(out) / 1900.0)


def f_tr(out, in_, ident):
    return _with(lambda e: e.transpose(out, in_, ident), 0.1)


def f_act(out, in_, func, bias=None, scale=None, accum=None):
    kw = {}
    if bias is not None:
        kw["bias"] = bias
    if scale is not None:
        kw["scale"] = scale
    if accum is not None:
        kw["accum_out"] = accum
    return _with(lambda e: e.activation(out=out, in_=in_, func=func, **kw), 0.25 + _fsz(out) / 1200.0)


def f_tt(out, in0, in1, op):
    return _with(lambda e: e.tensor_tensor(out=out, in0=in0, in1=in1, op=op), 0.2 + _fsz(out) / 960.0)


def f_stt(out, in0, scalar, in1, op0, op1):
    return _with(lambda e: e.scalar_tensor_tensor(out=out, in0=in0, scalar=scalar, in1=in1, op0=op0, op1=op1),
                 0.2 + _fsz(out) / 960.0)


def f_ts(out, in0, s1, s2, op0, op1):
    return _with(lambda e: e.tensor_scalar(out=out, in0=in0, scalar1=s1, scalar2=s2, op0=op0, op1=op1),
                 0.2 + _fsz(out) / 960.0)


def f_ts1(out, in0, s1, op0):
    return _with(lambda e: e.tensor_single_scalar(out=out, in_=in0, scalar=s1, op=op0), 0.2 + _fsz(out) / 960.0)


def f_red(out, in_):
    return _with(lambda e: e.reduce_sum(out=out, in_=in_, axis=mybir.AxisListType.X), 0.2 + _fsz(in_) / 960.0)


def f_cp(out, in_):
    return _with(lambda e: e.tensor_copy(out=out, in_=in_), 0.2 + _fsz(out) / 960.0)


def f_rcp(out, in_):
    return _with(lambda e: e.reciprocal(out=out, in_=in_), 0.2 + _fsz(out) / 160.0)


def f_ms(out, val):
    return _with(lambda e: e.memset(out, val), 0.2 + _fsz(out) / 960.0)


def f_dma(out, in_):
    return _with(lambda e: e.dma_start(out=out, in_=in_), 2.0 + 128 * _fsz(out) * 3.0 / 200e3)


PIECES = (["b", "a", "q", "k", "v", "g", "xk0", "xk1", "xv0", "xv1", "o0", "o1",
           "xq0", "xq1", "xo0", "xo1"]
          + [n for jp in range(4) for n in ("u%d" % (2 * jp), "u%d" % (2 * jp + 1),
                                           "d%d" % (2 * jp), "d%d" % (2 * jp + 1))])
PID = {n: i for i, n in enumerate(PIECES)}
TILE_SEQ = [n for n in PIECES if n[:2] not in ("xk", "xv")]


def build_program(NT):
    S = NT * T
    nc = bass.Bass("TRN2", target_bir_lowering=False)

    def din(name, shape, dt=F32):
        return nc.dram_tensor(name, shape, dt, kind="ExternalInput").ap()

    xT = din("xT", [D, S])
    memT = din("memT", [D, NMEM])
    w_in = din("w_in", [D, 3072])
    w_out = din("w_out", [D, D])
    w_xq = din("w_xq", [D, D])
    w_xk = din("w_xk", [D, D])
    w_xv = din("w_xv", [D, D])
    w_xo = din("w_xo", [D, D])
    w_up = din("w_up", [D, 4096])
    w_down = din("w_down", [4096, D])
    vec_d = din("vec", [128, NV])
    gnt_d = din("gnt", [128, 2, 512])
    qdt_d = din("qdt", [128, 4, 128])
    mask_d = din("maskt", [128, 4, 128])
    cos_d = din("cost", [128, S])
    sin_d = din("sint", [128, S])
    idn_d = din("idn", [128, 128])
    prm_d = din("prm", [128, 128])
    one_d = din("one", [128, 128])
    outT = nc.dram_tensor("outT", [D, S], F32, kind="ExternalOutput").ap()
    wscr = nc.dram_tensor("wscr", [len(PIECES), 128, 4096], BF16).ap()

    def wsrc(name):
        def cols(w, c0):
            return w[:, c0:c0 + 512].rearrange("(c p) n -> p c n", p=128)
        if name in ("b", "a", "q", "k", "v", "g"):
            c0 = {"a": 0, "b": 512, "q": 1024, "k": 1536, "v": 2048, "g": 2560}[name]
            return cols(w_in, c0)
        if name[0] == "o":
            return cols(w_out, 512 * int(name[1:]))
        if name[:2] == "xq":
            return cols(w_xq, 512 * int(name[2:]))
        if name[:2] == "xk":
            return cols(w_xk, 512 * int(name[2:]))
        if name[:2] == "xv":
            return cols(w_xv, 512 * int(name[2:]))
        if name[:2] == "xo":
            return cols(w_xo, 512 * int(name[2:]))
        if name[0] == "u":
            return cols(w_up, 512 * int(name[1:]))
        if name[0] == "d":
            r0 = 512 * int(name[1:])
            return w_down[r0:r0 + 512, :].rearrange("(c p) n -> p c n", p=128)
        raise KeyError(name)

    with ExitStack() as es:
        def sb(name, shape, dt):
            return es.enter_context(nc.sbuf_tensor(name, shape, dt))

        RING = [sb("ring%d" % i, [128, 8, 512], BF16) for i in range(4)]
        RA = sb("RA", [128, 8, 512], F32)
        XB = sb("XB", [128, 8, 512], BF16)
        RB = sb("RB", [128, 8, 512], BF16)
        CATS = [sb("CAT%d" % i, [128, 8, 512], BF16) for i in range(2)]
        H = sb("H", [128, 8, 512], BF16)
        ZB = [sb("ZB%d" % i, [128, 512], BF16) for i in range(4)]
        ZSQ = [sb("ZSQ%d" % i, [128, 512], BF16) for i in range(4)]
        U = sb("U", [128, 4, 542], F32)
        SGB = sb("SGB", [128, 4, 512], F32)
        QBT = [sb("QBT%d" % i, [128, 512], BF16) for i in range(2)]
        T1 = [sb("T1_%d" % i, [128, 512], F32) for i in range(2)]
        T2 = [sb("T2_%d" % i, [128, 512], F32) for i in range(2)]
        QP = sb("QP", [128, 4, 512], BF16)
        QD = sb("QD", [128, 4, 512], BF16)
        KP = sb("KP", [128, 4, 512], BF16)
        KD = [sb("KD%d" % i, [128, 4, 128], BF16) for i in range(2)]
        V = sb("V", [128, 4, 512], BF16)
        SG = sb("SG", [128, 4, 512], BF16)
        CF = sb("CF", [128, 4, 512], F32)
        STATS = {"b": [sb("%s_b" % n, [128, 512], F32) for n in ("MEAN", "MSQ")],
                 "f": [T1[0], T2[0]]}
        SKEY = {"b": (("MEAN", "b"), ("MSQ", "b")), "f": (("T1", 0), ("T2", 0))}
        for w in ("f", "b"):
            STATS[w].append(STATS[w][1])
        PT = [sb("PT%d" % i, [128, 128], BF16) for i in range(4)]
        ST = sb("ST", [128, 4, 128], F32)
        STB = sb("STB", [128, 4, 128], BF16)
        YN = [sb("YN0", [128, 512], F32)] * 2
        RT = [sb("RT%d" % i, [128, 512], BF16) for i in range(2)]
        GNS = [sb("GNS%d" % i, [128, 32], F32) for i in range(2)]
        PX = [sb("PX%d" % i, [128, 2, 512], BF16) for i in range(2)]
        RCP = [sb("RCP0", [128, 512], F32)] * 2
        RL = [sb("RL0", [128, 512], F32)]
        COS = sb("COS", [128, 512], F32)
        SIN = sb("SIN", [128, 512], F32)
        QDT = sb("QDT", [128, 4, 128], F32)
        MASKT = sb("MASKT", [128, 4, 128], F32)
        IDN = sb("IDN", [128, 128], BF16)
        PRM = sb("PRM", [128, 128], BF16)
        ONE = sb("ONE", [128, 128], BF16)
        VEC = sb("VEC", [128, NV], F32)
        GNT = sb("GNT", [128, 2, 512], F32)
        KXT = sb("KXT", [128, 8, 256], BF16)
        VX = sb("VX", [128, 2, 1024], BF16)

        PSF = [es.enter_context(nc.psum_tensor("psf%d" % i, [128, 512], F32)) for i in range(NPSF)]
        PSB = [es.enter_context(nc.psum_tensor("psb%d" % i, [128, 1024], BF16)) for i in range(8 - NPSF)]
        if len(PSB) == 1:
            PSB = [PSB[0], PSB[0]]

        sems = {e: es.enter_context(nc.semaphore("s_" + e)) for e in ("pe", "act", "dve", "pool")}
        CKEYS = ["c_idn", "c_prm", "c_one", "c_vec", "c_gnt", "c_qdt", "c_mask"]
        K_IDN, K_PRM, K_ONE, K_VEC, K_GNT, K_QDT, K_MASK = CKEYS
        dkeys = ([("ring", i) for i in range(4)] + [("wb", i) for i in range(len(PIECES))]
                 + CKEYS + ["mem", "xb", "cos", "sin"] + [("ra", c) for c in range(8)] + [("out", c) for c in range(8)])
        dsems = {k: es.enter_context(nc.semaphore("d%d" % i)) for i, k in enumerate(dkeys)}
        block = es.enter_context(nc.Block())

        def vcol(c):
            return VEC[:, c:c + 1]
        C_CONVB, C_CG, C_CB = 0, 4, 8
        C_LN = {1: (12, 20), 2: (28, 36), 3: (44, 52)}
        C_KDC, C_EPS, C_CW = 60, 64, 65

        def bcast(t, off, n):
            return bass.AP(t, off, [[32, 128], [1, 4], [0, n]])

        class Banks:
            def __init__(self):
                self.i = 0
                self.held = set()

            def next(self):
                while True:
                    b = self.i % NPSF
                    self.i += 1
                    if b not in self.held:
                        return PSF[b], ("psf", b)

            def hold(self):
                t, k = self.next()
                self.held.add(k[1])
                return t, k

            def unhold(self, k):
                self.held.discard(k[1])

        class RecWS:
            def __init__(self):
                self.seq = []

            def start(self):
                pass

            def get(self, name):
                self.seq.append(name)
                return 0

            def release(self, slot):
                pass

        class WStream:
            def __init__(self, P, seq):
                self.P = P
                self.seq = seq
                self.n_issued = 0
                self.n_got = 0
                self.free = [0, 1, 2, 3]
                self.where = {}

            def _issue(self):
                if self.n_issued >= len(self.seq) or not self.free:
                    return
                i = self.n_issued
                slot = self.free.pop(0)
                pid = PID[self.seq[i]]
                self.P.add("sp", f_dma(RING[slot][:], wscr[pid].rearrange("p (c n) -> p c n", c=8)),
                           reads=[("wb", pid)], writes=[("ring", slot)], dma=("ring", slot))
                self.where[i] = slot
                self.n_issued += 1

            def start(self):
                for _ in range(4):
                    self._issue()

            def get(self, name):
                i = self.n_got
                assert self.seq[i] == name, (self.seq[i], name)
                assert i in self.where, "ring exhausted"
                self.n_got += 1
                return self.where.pop(i)

            def release(self, slot):
                self.free.append(slot)
                self._issue()

        def emit_all(P, WS, picks, replay):
            PS = Banks()
            ri = {}

            def nxt(name, n):
                v = ri.get(name, 0)
                ri[name] = v + 1
                return v % n

            def load_xb(t):
                P.add("pool", f_dma(XB[:], xT[:, t * T:(t + 1) * T].rearrange("(c p) s -> p c s", p=128)),
                      writes=[("XB", c) for c in range(8)], dma="xb")

            def load_ra(t, c):
                P.add("sp", f_dma(RA[:, c, :], xT[c * 128:(c + 1) * 128, t * T:(t + 1) * T]),
                      writes=[("RA", c)], dma=("ra", c))

            def load_cs(t):
                P.add("pool", f_dma(COS[:], cos_d[:, t * T:(t + 1) * T]), writes=["COS"], dma="cos")
                P.add("pool", f_dma(SIN[:], sin_d[:, t * T:(t + 1) * T]), writes=["SIN"], dma="sin")

            for dst, src, key in ((IDN, idn_d, K_IDN), (PRM, prm_d, K_PRM), (ONE, one_d, K_ONE), (VEC, vec_d, K_VEC),
                                  (GNT, gnt_d, K_GNT), (QDT, qdt_d, K_QDT), (MASKT, mask_d, K_MASK)):
                P.add("pool", f_dma(dst[:], src), writes=[key], dma=key)
            MB = H[:, :, 0:256]
            P.add("pool", f_dma(MB, memT.rearrange("(c p) m -> p c m", p=128)), writes=["MB"], dma="mem")
            load_xb(0)
            load_cs(0)
            for c in range(8):
                load_ra(0, c)
            for name in PIECES:
                pid = PID[name]
                src = wsrc(name)
                c = src.shape[1]
                P.add("pool", f_dma(wscr[pid].rearrange("p (c n) -> p c n", c=c), src),
                      writes=[("wb", pid)], dma=("wb", pid))
            WS.start()
            P.add("dve", f_ms(U[:, :, 0:30], 0.0), writes=[("U", c) for c in range(4)])
            P.add("dve", f_ms(ST[:], 0.0), writes=["ST"])
            P.add("dve", f_ms(STB[:], 0.0), writes=["STB"])


            def mem_kv():
                for nm in ("xk0", "xk1"):
                    s_ = WS.get(nm)
                    for o4 in range(4):
                        oc = 4 * int(nm[2]) + o4
                        bt, bk = PS.next()
                        for kc in range(8):
                            P.add("pe", f_mm(bt[:, 0:256], RING[s_][:, kc, o4 * 128:(o4 + 1) * 128], H[:, kc, 0:256],
                                             kc == 0, kc == 7),
                                  reads=[("ring", s_), "MB"], writes=[bk])
                        P.add("act", f_act(KXT[:, oc, :], bt[:, 0:256], AF.Copy), writes=[bk, "KXT"])
                    WS.release(s_)
                for half, nm in enumerate(("xv0", "xv1")):
                    s_ = WS.get(nm)
                    for mc in range(2):
                        bt, bk = PS.next()
                        for kc in range(8):
                            P.add("pe", f_mm(bt[:], H[:, kc, mc * 128:(mc + 1) * 128], RING[s_][:, kc, :], kc == 0, kc == 7),
                                  reads=[("ring", s_), "MB"], writes=[bk])
                        P.add("act", f_act(VX[:, mc, half * 512:(half + 1) * 512], bt[:], AF.Copy), writes=[bk, "VX"])
                    WS.release(s_)
                P.add("dve", f_ms(H[:, 0, 0:2], 0.0), writes=["MB"] + [("H", c) for c in range(8)])

            pend_stats = []

            def flush_stats(keep=0):
                while len(pend_stats) > keep:
                    pend_stats.pop(0)()

            def ln_stat_chunk(src, src_key, sum_b, sq_b, first, last, defer=0):
                i = nxt("z", 4)
                P.add("act", f_act(ZB[i][:], src, AF.Copy), reads=[src_key], writes=[("ZB", i)])
                P.add("act", f_act(ZSQ[i][:], src, AF.Square), reads=[src_key], writes=[("ZSQ", i)])

                def mms():
                    P.add("pe", f_mm(sum_b[0][:], ONE[:], ZB[i][:], first, last), reads=[("ZB", i), K_ONE],
                          writes=[sum_b[1]])
                    P.add("pe", f_mm(sq_b[0][:], ONE[:], ZSQ[i][:], first, last), reads=[("ZSQ", i), K_ONE],
                          writes=[sq_b[1]])
                pend_stats.append(mms)
                flush_stats(defer)

            def ln_post(w, sum_b, sq_b, dd):
                MEAN, MSQ, RSTD = STATS[w]
                km, kq = SKEY[w]
                kr = kq
                inv = 1.0 / dd
                P.add("act", f_act(MEAN[:], sum_b[0][:], AF.Copy, scale=inv), writes=[sum_b[1], km])
                P.add("dve", f_tt(MSQ[:], MEAN[:], MEAN[:], ALU.mult), reads=[km], writes=[kq])
                P.add("dve", f_stt(MSQ[:], sq_b[0][:], inv, MSQ[:], ALU.mult, ALU.subtract),
                      writes=[sq_b[1], kq])
                P.add("act", f_act(RSTD[:], MSQ[:], AF.Ln, bias=vcol(C_EPS), scale=1.0),
                      reads=[kq, K_VEC], writes=[kr])
                P.add("act", f_act(RSTD[:], RSTD[:], AF.Exp, scale=-0.5), writes=[kr])
                PS.unhold(sum_b[1])
                PS.unhold(sq_b[1])

            def ln_norm_chunk(which, oc, make_rb):
                MEAN, MSQ, RSTD = STATS["b"]
                cg, cb = C_LN[which]
                e1 = "pool" if (LNMODE == 1 and oc % 2 == 1) else "dve"
                e2 = "dve" if (LNMODE == 2 and oc % 2 == 1) else "pool"
                P.add(e1, f_tt(RA[:, oc, :], RA[:, oc, :], MEAN[:], ALU.subtract),
                      reads=[("MEAN", "b")], writes=[("RA", oc)])
                P.add(e2, f_tt(RA[:, oc, :], RA[:, oc, :], RSTD[:], ALU.mult),
                      reads=[("MSQ", "b")], writes=[("RA", oc)])
                if make_rb:
                    P.add("act", f_act(RB[:, oc, :], RA[:, oc, :], AF.Identity, bias=vcol(cb + oc), scale=vcol(cg + oc)),
                          reads=[K_VEC, ("RA", oc)], writes=[("RB", oc)])
                P.add("act", f_act(RA[:, oc, :], RA[:, oc, :], AF.Identity, bias=vcol(cb + oc), scale=vcol(cg + oc)),
                      reads=[K_VEC], writes=[("RA", oc)])

            def residual_ln(which, pieces, rhs_buf, rhs_key, make_rb):
                sum_b = PS.hold()
                sq_b = PS.hold()
                s_ = None
                for oc in range(8):
                    if oc % 4 == 0:
                        s_ = WS.get(pieces[oc // 4])
                    bt, bk = PS.next()
                    for kc in range(8):
                        P.add("pe", f_mm(bt[:], RING[s_][:, kc, (oc % 4) * 128:(oc % 4 + 1) * 128], rhs_buf[:, kc, :],
                                         kc == 0, kc == 7),
                              reads=[("ring", s_), (rhs_key, kc)], writes=[bk])
                    P.add("dve", f_stt(RA[:, oc, :], RA[:, oc, :], ALPHA, bt[:], ALU.mult, ALU.add),
                          writes=[bk, ("RA", oc)])
                    ln_stat_chunk(RA[:, oc, :], ("RA", oc), sum_b, sq_b, oc == 0, oc == 7, defer=2)
                    if oc % 4 == 3:
                        WS.release(s_)
                    yield
                flush_stats()
                ln_post("b", sum_b, sq_b, float(D))
                for oc in range(8):
                    ln_norm_chunk(which, oc, make_rb)
                    if oc % 2 == 1:
                        yield

            convq = []
            cat_free = [-1]

            def front(t):
                P.tile["f"] = t
                CAT = CATS[t % 2]
                ck = "CAT%d" % (t % 2)
                P.ph["f"] = "glu"
                s_ = WS.get("b")
                for cc in range(4):
                    b1, k1 = PS.next()
                    for kc in range(8):
                        P.add("pe", f_mm(b1[:], RING[s_][:, kc, cc * 128:(cc + 1) * 128], XB[:, kc, :], kc == 0, kc == 7),
                              reads=[("ring", s_), ("XB", kc)], writes=[k1])
                    P.add("act", f_act(SGB[:, cc, :], b1[:], AF.Sigmoid), writes=[k1, ("SGB", cc)])
                    yield
                WS.release(s_)
                s_ = WS.get("a")
                for cc in range(4):
                    b2, k2 = PS.next()
                    for kc in range(8):
                        P.add("pe", f_mm(b2[:], RING[s_][:, kc, cc * 128:(cc + 1) * 128], XB[:, kc, :], kc == 0, kc == 7),
                              reads=[("ring", s_), ("XB", kc)], writes=[k2])
                    P.add("dve", f_tt(U[:, cc, 30:542], b2[:], SGB[:, cc, :], ALU.mult),
                          reads=[("SGB", cc)], writes=[k2, ("U", cc)])
                    yield
                WS.release(s_)

                for cc in range(4):
                    convq.append(lambda cc=cc: P.add(
                        "pool", f_ts(CF[:, cc, :], U[:, cc, 0:512], vcol(C_CW + cc * 31), vcol(C_CONVB + cc),
                                     ALU.mult, ALU.add),
                        reads=[("U", cc), K_VEC], writes=[("CF", cc)]))
                    for k in range(1, 31):
                        convq.append(lambda cc=cc, k=k: P.add(
                            "dve", _with(f_stt(CF[:, cc, :], U[:, cc, k:k + 512], vcol(C_CW + cc * 31 + k), CF[:, cc, :],
                                               ALU.mult, ALU.add), 0.73 * CONVSCALE),
                            reads=[("U", cc)], writes=[("CF", cc)]))
                    convq.append(lambda cc=cc: P.add("pool", f_cp(U[:, cc, 0:30], U[:, cc, 512:542]),
                                                     writes=[("U", cc)]))

                def drain(n):
                    cur = P.cur
                    for _ in range(min(n, len(convq))):
                        convq.pop(0)()
                    P.cur = cur

                P.ph["f"] = "qk"
                pend_rot = []
                for nm in ("q", "k"):
                    s_ = WS.get(nm)
                    sc = 128.0 ** -0.5 if nm == "q" else 1.0
                    dest, dkey = (QP, "QP") if nm == "q" else (KP, "KP")
                    for h in range(4):
                        b1, k1 = PS.hold()
                        for kc in range(8):
                            P.add("pe", f_mm(b1[:], RING[s_][:, kc, h * 128:(h + 1) * 128], XB[:, kc, :], kc == 0, kc == 7),
                                  reads=[("ring", s_), ("XB", kc)], writes=[k1])
                        i = nxt("rot", 2)
                        P.add("act", f_act(QBT[i][:], b1[:], AF.Copy, scale=sc), writes=[k1, ("QBT", i)])

                        def rot(b1=b1, k1=k1, i=i, sc=sc, dest=dest, dkey=dkey, h=h, nm=nm):
                            b2, k2 = PS.next()
                            P.add("pe", f_mm(b2[:], PRM[:], QBT[i][:], True, True), reads=[("QBT", i), K_PRM], writes=[k2])
                            P.add("dve", f_stt(T1[i][:], b1[:], sc, COS[:], ALU.mult, ALU.mult),
                                  reads=["COS"], writes=[k1, ("T1", i)])
                            PS.unhold(k1)
                            P.add("dve", f_tt(T2[i][:], b2[:], SIN[:], ALU.mult), reads=["SIN"], writes=[k2, ("T2", i)])
                            P.add("pool", f_tt(dest[:, h, :], T1[i][:], T2[i][:], ALU.add),
                                  reads=[("T1", i), ("T2", i)], writes=[(dkey, h)])
                            if nm == "q":
                                for j in range(4):
                                    P.add("pool", f_tt(QD[:, h, j * 128:(j + 1) * 128], QP[:, h, j * 128:(j + 1) * 128],
                                                       QDT[:, h, :], ALU.mult),
                                          reads=[("QP", h), K_QDT], writes=[("QD", h)])
                        pend_rot.append(rot)
                        while len(pend_rot) > 1:
                            pend_rot.pop(0)()
                        drain(FD_QK)
                        yield
                    WS.release(s_)
                while pend_rot:
                    pend_rot.pop(0)()
                yield
                if t + 1 < NT:
                    load_cs(t + 1)

                P.ph["f"] = "vg"
                for nm in ("v", "g"):
                    s_ = WS.get(nm)
                    for j in range(4):
                        b1, k1 = PS.next()
                        for kc in range(8):
                            P.add("pe", f_mm(b1[:], XB[:, kc, j * 128:(j + 1) * 128], RING[s_][:, kc, :], kc == 0, kc == 7),
                                  reads=[("ring", s_), ("XB", kc)], writes=[k1])
                        if nm == "v":
                            P.add("act", f_act(V[:, j, :], b1[:], AF.Copy), writes=[k1, ("V", j)])
                        else:
                            P.add("act", f_act(SG[:, j, :], b1[:], AF.Silu), writes=[k1, ("SG", j)])
                        drain(FD_VG)
                        yield
                    WS.release(s_)
                if t + 1 < NT:
                    load_xb(t + 1)
                if t == 0:
                    mem_kv()
                    yield

                P.ph["f"] = "ret"
                pend_tail = []
                for j in range(4):
                    js = slice(j * 128, (j + 1) * 128)
                    bs, ks = PS.hold()
                    for h in range(4):
                        P.add("pe", f_mm(bs[:, h * 128:(h + 1) * 128], KP[:, h, js], QP[:, h, js], True, True),
                              reads=[("KP", h), ("QP", h)], writes=[ks])
                    for h in range(4):
                        P.add("pe", f_tr(PSB[0][:, h * 128:(h + 1) * 128], KP[:, h, js], IDN[:]),
                              reads=[("KP", h), K_IDN], writes=[("psb", 0)])
                    kdi = nxt("kd", 2)
                    for h in range(4):
                        P.add("act", f_act(KD[kdi][:, h, :], PSB[0][:, h * 128:(h + 1) * 128], AF.Copy,
                                           scale=vcol(C_KDC + h)),
                              reads=[K_VEC], writes=[("psb", 0), ("KD", kdi)])
                    drain(FD_RET)
                    yield
                    by, ky = PS.hold()
                    for h in range(4):
                        pi = nxt("pt", 4)
                        P.add("dve", f_tt(PT[pi][:], bs[:, h * 128:(h + 1) * 128], MASKT[:, h, :], ALU.mult),
                              reads=[K_MASK], writes=[ks, ("PT", pi)])
                        P.add("pe", f_mm(by[:, h * 128:(h + 1) * 128], PT[pi][:], V[:, j, h * 128:(h + 1) * 128],
                                         True, False),
                              reads=[("PT", pi), ("V", j)], writes=[ky])
                        P.add("pe", f_mm(by[:, h * 128:(h + 1) * 128], QD[:, h, js], STB[:, h, :], False, True),
                              reads=[("QD", h), "STB"], writes=[ky])
                    PS.unhold(ks)
                    drain(FD_RET)
                    yield
                    bu, ku = PS.next()
                    for h in range(4):
                        P.add("pe", f_mm(bu[:, h * 128:(h + 1) * 128], KD[kdi][:, h, :], V[:, j, h * 128:(h + 1) * 128],
                                         True, True),
                              reads=[("KD", kdi), ("V", j)], writes=[ku])
                    for h in range(4):
                        P.add("dve", f_stt(ST[:, h, :], ST[:, h, :], GAM[h] ** 128, bu[:, h * 128:(h + 1) * 128],
                                           ALU.mult, ALU.add),
                              writes=[ku, "ST"])
                    P.add("act", f_act(STB[:], ST[:], AF.Copy), reads=["ST"], writes=["STB"])
                    while pend_tail:
                        pend_tail.pop(0)()
                    drain(FD_RET)
                    yield
                    yi = nxt("yn", 2)
                    g = GNS[yi]
                    gk = ("GNS", yi)
                    by3 = by[:].rearrange("p (h e) -> p h e", h=4)
                    yn3 = YN[yi][:].rearrange("p (h e) -> p h e", h=4)
                    P.add("act", f_act(YN[yi][:], by[:], AF.Square), writes=[ky, ("YN", 0)])
                    P.add("dve", f_red(g[:, 0:4], by3), writes=[ky, gk])
                    P.add("dve", f_red(g[:, 4:8], yn3), reads=[("YN", 0)], writes=[gk])
                    P.add("dve", f_ts1(g[:, 8:12], g[:, 0:4], 1.0 / 128, ALU.mult), writes=[gk])
                    P.add("dve", f_tt(g[:, 12:16], g[:, 8:12], g[:, 8:12], ALU.mult), writes=[gk])
                    P.add("dve", f_stt(g[:, 16:20], g[:, 4:8], 1.0 / 128, g[:, 12:16], ALU.mult, ALU.subtract),
                          writes=[gk])
                    P.add("act", f_act(g[:, 20:24], g[:, 16:20], AF.Sqrt, bias=vcol(C_EPS), scale=1.0),
                          reads=[K_VEC], writes=[gk])
                    P.add("dve", f_rcp(g[:, 20:24], g[:, 20:24]), writes=[gk])
                    P.add("dve", f_tt(yn3, by3, bcast(g, 8, 128), ALU.subtract), reads=[gk], writes=[ky, ("YN", 0)])
                    P.add("dve", f_tt(yn3, yn3, bcast(g, 20, 128), ALU.mult), reads=[gk], writes=[("YN", 0)])
                    P.add("pool", f_tt(YN[yi][:], YN[yi][:], GNT[:, 0, :], ALU.mult), reads=[K_GNT], writes=[("YN", 0)])
                    P.add("pool", f_tt(YN[yi][:], YN[yi][:], GNT[:, 1, :], ALU.add), reads=[K_GNT], writes=[("YN", 0)])
                    P.add("pool", f_tt(RT[yi][:], YN[yi][:], SG[:, j, :], ALU.mult), reads=[("YN", 0), ("SG", j)],
                          writes=[("RT", yi)])
                    def tail(yi=yi, js=js):
                        for h in range(4):
                            P.add("pe", f_tr(PSB[1][:, h * 128:(h + 1) * 128], RT[yi][:, h * 128:(h + 1) * 128], IDN[:]),
                                  reads=[("RT", yi), K_IDN], writes=[("psb", PSB1K)])
                        P.add("act", f_act(CAT[:, 4:8, js], PSB[1][:, 0:512].rearrange("p (h n) -> p h n", h=4), AF.Copy),
                              writes=[("psb", PSB1K)] + [(ck, 4 + h) for h in range(4)])
                    pend_tail.append(tail)
                    PS.unhold(ky)
                    drain(FD_RET)
                    yield
                while pend_tail:
                    pend_tail.pop(0)()
                drain(len(convq))
                yield
                P.ph["f"] = "convln"
                sum_b = PS.hold()
                sq_b = PS.hold()
                for cc in range(4):
                    ln_stat_chunk(CF[:, cc, :], ("CF", cc), sum_b, sq_b, cc == 0, cc == 3)
                ln_post("f", sum_b, sq_b, 512.0)
                MEAN, MSQ, RSTD = STATS["f"]
                for cc in range(4):
                    P.add("dve", f_tt(CF[:, cc, :], CF[:, cc, :], MEAN[:], ALU.subtract), reads=[SKEY["f"][0]],
                          writes=[("CF", cc)])
                    P.add("pool", f_tt(CF[:, cc, :], CF[:, cc, :], RSTD[:], ALU.mult), reads=[SKEY["f"][1]],
                          writes=[("CF", cc)])
                    P.add("act", f_act(CAT[:, cc, :], CF[:, cc, :], AF.Silu, bias=vcol(C_CB + cc), scale=vcol(C_CG + cc)),
                          reads=[("CF", cc), K_VEC], writes=[(ck, cc)])
                yield

            def back(t):
                P.tile["b"] = t
                CAT = CATS[t % 2]
                ck = "CAT%d" % (t % 2)
                P.ph["b"] = "wout"
                yield from residual_ln(1, ("o0", "o1"), CAT, ck, True)
                P.ph["b"] = "xattn"
                s_ = None
                for oc in range(8):
                    if oc % 4 == 0:
                        s_ = WS.get("xq%d" % (oc // 4))
                    b1, k1 = PS.next()
                    for kc in range(8):
                        P.add("pe", f_mm(b1[:], RING[s_][:, kc, (oc % 4) * 128:(oc % 4 + 1) * 128], RB[:, kc, :],
                                         kc == 0, kc == 7),
                              reads=[("ring", s_), ("RB", kc)], writes=[k1])
                    P.add("act", f_act(CAT[:, oc, :], b1[:], AF.Copy), writes=[k1, (ck, oc)])
                    if oc % 4 == 3:
                        WS.release(s_)
                    yield
                for h in range(4):
                    pi = nxt("px", 2)
                    for mc in range(2):
                        b1, k1 = PS.next()
                        for dc in range(2):
                            P.add("pe", f_mm(b1[:], KXT[:, 2 * h + dc, mc * 128:(mc + 1) * 128], CAT[:, 2 * h + dc, :],
                                             dc == 0, dc == 1),
                                  reads=["KXT", (ck, 2 * h + dc)], writes=[k1])
                        P.add("act", f_act(PX[pi][:, mc, :], b1[:], AF.Exp, scale=1.0 / 16.0), writes=[k1, ("PX", pi)])
                    yield
                    b1, k1 = PS.next()
                    for mc in range(2):
                        P.add("pe", f_mm(b1[:], ONE[:], PX[pi][:, mc, :], mc == 0, mc == 1),
                              reads=[("PX", pi), K_ONE], writes=[k1])
                    P.add("act", f_act(RCP[pi][:], b1[:], AF.Ln), writes=[k1, ("RCP", 0)])
                    P.add("act", f_act(RCP[pi][:], RCP[pi][:], AF.Exp, scale=-1.0), writes=[("RCP", 0)])
                    for dc in range(2):
                        b2, k2 = PS.next()
                        for mc in range(2):
                            P.add("pe", f_mm(b2[:], VX[:, mc, (2 * h + dc) * 128:(2 * h + dc + 1) * 128],
                                             PX[pi][:, mc, :], mc == 0, mc == 1),
                                  reads=["VX", ("PX", pi)], writes=[k2])
                        P.add("dve", f_tt(CAT[:, 2 * h + dc, :], b2[:], RCP[pi][:], ALU.mult),
                              reads=[("RCP", 0)], writes=[k2, (ck, 2 * h + dc)])
                    yield
                P.ph["b"] = "wxo"
                yield from residual_ln(2, ("xo0", "xo1"), CAT, ck, True)
                cat_free[0] = t
                P.ph["b"] = "mlp"

                for jp in range(4):
                    if jp == 3:
                        sum_b = PS.hold()
                        sq_b = PS.hold()
                    s_ = None
                    for hc in range(8):
                        if hc % 4 == 0:
                            s_ = WS.get("u%d" % (2 * jp + hc // 4))
                        b1, k1 = PS.next()
                        for kc in range(8):
                            P.add("pe", f_mm(b1[:], RING[s_][:, kc, (hc % 4) * 128:(hc % 4 + 1) * 128], RB[:, kc, :],
                                             kc == 0, kc == 7),
                                  reads=[("ring", s_), ("RB", kc)], writes=[k1])
                        i = 0
                        P.add("act", f_act(RL[i][:], b1[:], AF.Relu), writes=[k1, ("RL", i)])
                        P.add("act", f_act(H[:, hc, :], RL[i][:], AF.Square), reads=[("RL", i)], writes=[("H", hc)])
                        if hc % 4 == 3:
                            WS.release(s_)
                        yield
                    sd = [WS.get("d%d" % (2 * jp)), WS.get("d%d" % (2 * jp + 1))]
                    for oc in range(8):
                        b1, k1 = PS.next()
                        for kc8 in range(8):
                            s_ = sd[kc8 // 4]
                            kk = kc8 % 4
                            P.add("pe", f_mm(b1[:], RING[s_][:, kk * 2 + oc // 4, (oc % 4) * 128:(oc % 4 + 1) * 128],
                                             H[:, kc8, :], kc8 == 0, kc8 == 7),
                                  reads=[("ring", s_), ("H", kc8)], writes=[k1])
                        if jp == 0:
                            P.add("dve", f_stt(RA[:, oc, :], RA[:, oc, :], ALPHA, b1[:], ALU.mult, ALU.add),
                                  writes=[k1, ("RA", oc)])
                        else:
                            P.add("dve", f_tt(RA[:, oc, :], RA[:, oc, :], b1[:], ALU.add), writes=[k1, ("RA", oc)])
                        if jp == 3:
                            ln_stat_chunk(RA[:, oc, :], ("RA", oc), sum_b, sq_b, oc == 0, oc == 7, defer=2)
                        yield
                    WS.release(sd[0])
                    WS.release(sd[1])
                flush_stats()
                ln_post("b", sum_b, sq_b, float(D))
                for oc in range(8):
                    ln_norm_chunk(3, oc, False)
                    P.add("sp", f_dma(outT[oc * 128:(oc + 1) * 128, t * T:(t + 1) * T], RA[:, oc, :]),
                          reads=[("RA", oc)], dma=("out", oc))
                    if oc >= 1 and t + 1 < NT:
                        load_ra(t + 1, oc - 1)
                    if oc % 2 == 1:
                        yield
                if t + 1 < NT:
                    load_ra(t + 1, 7)

            def chain_front():
                for t in range(NT):
                    while t >= 2 and (back_done[0] if NOAHEAD else cat_free[0]) < t - 2:
                        yield "blocked"
                    yield from front(t)
                    front_done[0] = t

            def chain_back():
                for t in range(NT):
                    while front_done[0] < t:
                        yield "blocked"
                    yield from back(t)
                    back_done[0] = t

            front_done = [-1]
            back_done = [-1]
            gf = chain_front()
            gb_ = chain_back()
            tf = tb = 0.0
            alive_f = alive_b = True
            blocked_f = blocked_b = False
            while alive_f or alive_b:
                can_b = alive_b and not blocked_b
                can_f = alive_f and not blocked_f
                if can_b and can_f:
                    pick_b = tb <= tf + BIAS
                elif can_b or can_f:
                    pick_b = can_b
                else:
                    blocked_f = blocked_b = False
                    continue
                if replay is not None:
                    pick_b = replay.pop(0)
                else:
                    picks.append(pick_b)
                P.mark = 0.0
                P.first = None
                P.cur = "b" if pick_b else "f"
                try:
                    r = next(gb_ if pick_b else gf)
                except StopIteration:
                    if pick_b:
                        alive_b = False
                    else:
                        alive_f = False
                    blocked_f = blocked_b = False
                    continue
                if r == "blocked":
                    if pick_b:
                        blocked_b = True
                    else:
                        blocked_f = True
                    continue
                blocked_f = blocked_b = False
                if pick_b:
                    tb = max(tb, P.mark)
                else:
                    tf = max(tf, P.mark)
            P.sim_total = max(P.free.values())

        FD_QK, FD_VG, FD_RET, BD = 6, 5, 2, 0
        BIAS = 5.0
        NOAHEAD = False
        CONVSCALE = 1.0
        rec = RecWS()
        picks = []
        emit_all(Prog(), rec, picks, None)
        P = Prog()
        P.dump = False
        emit_all(P, WStream(P, rec.seq), None, list(picks))
        P.emit(block, sems, dsems, {"sp": [("out", c) for c in range(8)]})
    return nc


def _constants(S):
    pos = np.arange(S, dtype=np.float32)
    inv = (np.float32(10000.0) ** (-np.linspace(0.0, 1.0, 64, dtype=np.float32))).astype(np.float32)
    ang = (pos[:, None] * inv[None, :]).astype(np.float32)
    cos = np.cos(ang).astype(np.float32).T
    sin = np.sin(ang).astype(np.float32).T
    cost = np.concatenate([cos, cos], axis=0)
    sint = np.concatenate([-sin, sin], axis=0)
    gam = np.array(GAM, dtype=np.float64)
    n = np.arange(128, dtype=np.float64)
    qdt = np.broadcast_to((gam[:, None] ** (n[None, :] + 1.0))[None], (128, 4, 128)).astype(np.float32)
    m = n[:, None]
    nn = n[None, :]
    maskt = np.stack([np.where(m <= nn, g ** (nn - m), 0.0) for g in gam], axis=1).astype(np.float32)
    kdc = np.stack([g ** (127.0 - n) for g in gam], axis=1).astype(np.float32)
    idn = np.eye(128, dtype=np.float32)
    prm = np.roll(idn, 64, axis=1).copy()
    one = np.ones((128, 128), dtype=np.float32)
    return (np.ascontiguousarray(cost), np.ascontiguousarray(sint), np.ascontiguousarray(qdt),
            np.ascontiguousarray(maskt), kdc, idn, prm, one)


def _fm(v, nchunk):
    return np.asarray(v, dtype=np.float32).reshape(nchunk, 128).T


def _prepare(inputs, S):
    cost, sint, qdt, maskt, kdc, idn, prm, one = _constants(S)
    vec = np.zeros((128, NV), dtype=np.float32)
    vec[:, 0:4] = _fm(inputs["conv_b"][0], 4)
    vec[:, 4:8] = _fm(inputs["conv_ln_g"][0], 4)
    vec[:, 8:12] = _fm(inputs["conv_ln_b"][0], 4)
    for i, nm in enumerate(("ln1_g", "ln1_b", "ln2_g", "ln2_b", "ln3_g", "ln3_b")):
        vec[:, 12 + 8 * i:20 + 8 * i] = _fm(inputs[nm][0], 8)
    vec[:, 60:64] = kdc
    vec[:, 64] = EPS
    cw = np.asarray(inputs["conv_w"][0], dtype=np.float32)
    vec[:, 65:65 + 124] = cw.T.reshape(4, 128, 31).transpose(1, 0, 2).reshape(128, 124)
    gnt = np.stack([np.broadcast_to(np.asarray(inputs["ret_gn_g"][0], np.float32)[None, :], (128, 512)),
                    np.broadcast_to(np.asarray(inputs["ret_gn_b"][0], np.float32)[None, :], (128, 512))], axis=1)
    shared = {
        "w_in": np.ascontiguousarray(inputs["w_in"][0], dtype=np.float32),
        "w_out": np.ascontiguousarray(inputs["w_out"][0], dtype=np.float32),
        "w_xq": np.ascontiguousarray(inputs["w_xq"][0], dtype=np.float32),
        "w_xk": np.ascontiguousarray(inputs["w_xk"][0], dtype=np.float32),
        "w_xv": np.ascontiguousarray(inputs["w_xv"][0], dtype=np.float32),
        "w_xo": np.ascontiguousarray(inputs["w_xo"][0], dtype=np.float32),
        "w_up": np.ascontiguousarray(inputs["w_up"][0], dtype=np.float32),
        "w_down": np.ascontiguousarray(inputs["w_down"][0], dtype=np.float32),
        "vec": vec, "gnt": np.ascontiguousarray(gnt), "qdt": qdt, "maskt": maskt,
        "cost": cost, "sint": sint, "idn": idn, "prm": prm, "one": one,
    }
    return shared


def run(inputs, S, n_cores):
    shared = _prepare(inputs, S)
    x = np.asarray(inputs["x"], dtype=np.float32)
    mem = np.asarray(inputs["mem"], dtype=np.float32)
    in_maps = []
    for b in range(n_cores):
        m = dict(shared)
        m["xT"] = np.ascontiguousarray(x[b, :S].T)
        m["memT"] = np.ascontiguousarray(mem[b].T)
        in_maps.append(m)
    nc = build_program(S // T)
    res = run_bass_kernel_spmd(nc, in_maps, core_ids=list(range(n_cores)))
    out = np.stack([np.ascontiguousarray(res.results[b]["outT"].T) for b in range(n_cores)], axis=0)
    return out.astype(np.float32)


def kernel(**inputs):
    return run(inputs, SEQ, NB)
```

```python
from contextlib import ExitStack

import numpy as np
import concourse.bass as bass
import concourse.mybir as mybir
from concourse.bass_utils import run_bass_kernel_spmd

F32 = mybir.dt.float32
BF16 = mybir.dt.bfloat16
AF = mybir.ActivationFunctionType
ALU = mybir.AluOpType

D = 1024
SEQ = 4096
NB = 8
NMEM = 256
T = 512
ALPHA = 2.0 ** 0.25
EPS = 1e-5
NV = 192
GAM = [1.0 - 2.0 ** (-5.0 - h) for h in range(4)]

ENGS = ("pe", "act", "dve", "pool", "sp")

ENG_SCALE = {}
NPSF = 7
PSB1K = 1 if NPSF == 6 else 0
LNMODE = 0


class _Op:
    __slots__ = ("eng", "fn", "signal", "waits", "dma", "sigval", "fin", "ph", "idx")

    def __init__(self, eng, fn, dma):
        self.eng = eng
        self.fn = fn
        self.dma = dma
        self.signal = False
        self.waits = []
        self.sigval = None
        self.fin = 0.0


class Prog:
    def __init__(self):
        self.ops = {e: [] for e in ENGS}
        self.last_w = {}
        self.readers = {}
        self.dma_counts = {}
        self.free = {e: 0.0 for e in ENGS}
        self.ph = {"f": "pro", "b": "pro"}
        self.tile = {"f": 0, "b": 0}
        self.spans = {}
        self.cur = "f"
        self.gaps = {}
        self.busy = {e: 0.0 for e in ENGS}
        self.mark = 0.0
        self.first = None

    def add(self, eng, fn, reads=(), writes=(), dma=None):
        op = _Op(eng, fn, dma)
        op.idx = len(self.ops[eng])
        if dma is not None:
            self.dma_counts[dma] = self.dma_counts.get(dma, 0) + 16
            op.sigval = self.dma_counts[dma]
        deps = []
        for k in reads:
            w = self.last_w.get(k)
            if w is not None:
                deps.append((w, k))
        for k in writes:
            w = self.last_w.get(k)
            if w is not None:
                deps.append((w, k))
            deps.extend((r, k) for r in self.readers.get(k, {}).values())
        t0 = self.free[eng]
        op.ph = self.ph[self.cur]
        crit = None
        for d, k in deps:
            lat = 0.0 if (d.eng == eng and d.dma is None) else 0.06
            if d.fin + lat > t0:
                t0 = d.fin + lat
                crit = d
        if crit is not None and t0 > self.free[eng]:
            gk = (eng, op.ph, crit.eng if crit.dma is None else "dma", crit.ph)
            self.gaps[gk] = self.gaps.get(gk, 0.0) + t0 - self.free[eng]
        best = {}
        for d, k in deps:
            if d is op:
                continue
            if d.dma is None and d.eng == eng:
                if eng == "pe":
                    continue
                if isinstance(k, tuple) and k[0] in ("psf", "psb"):
                    continue
            g = ("dma", d.dma) if d.dma is not None else ("eng", d.eng)
            cur = best.get(g)
            if cur is None or (d.sigval > cur.sigval if d.dma is not None else d.idx > cur.idx):
                best[g] = d
        for d in best.values():
            op.waits.append(d)
            d.signal = True
        g_self = ("dma", dma) if dma is not None else ("eng", eng)
        for k in reads:
            self.readers.setdefault(k, {})[g_self] = op
        for k in writes:
            self.last_w[k] = op
            self.readers[k] = {}
        dur = getattr(fn, "dur", 0.5)
        if eng == "pool" and dma is None:
            dur = 2.0 * dur
        if dma is None:
            dur *= ENG_SCALE.get(eng, 1.0)
        op.fin = t0 + dur
        self.busy[eng] += dur
        if self.cur == "b" and self.tile["b"] == 3 and op.ph == "wout" and getattr(self, "dump", False) and eng != "sp":
            print("    OP %-4s t0=%8.2f fin=%8.2f dur=%5.2f qwait=%6.2f crit=%s" % (
                eng, t0, op.fin, dur, t0 - self.free[eng] if False else 0.0,
                (crit.eng + "/" + crit.ph + "@%.1f" % crit.fin) if crit is not None else "-"))
        sk = (self.cur, self.tile[self.cur], op.ph)
        sp = self.spans.get(sk)
        if sp is None:
            self.spans[sk] = [t0, op.fin, 0.0]
            sp = self.spans[sk]
        sp[1] = max(sp[1], op.fin)
        if eng == "pe":
            sp[2] += dur
        self.free[eng] = t0 + (0.1 if dma is not None else dur)
        if op.fin > self.mark:
            self.mark = op.fin
        if self.first is None:
            self.first = op.fin
        self.ops[eng].append(op)
        return op

    def emit(self, block, sems, dma_sems, final_waits):
        for e in ENGS:
            c = 0
            for op in self.ops[e]:
                if op.dma is None and op.signal:
                    c += 1
                    op.sigval = c
            self.sigcount = getattr(self, "sigcount", {})
            self.sigcount[e] = (c, len(self.ops[e]))
        prog = self

        def run(eng_name):
            def body(eng):
                known = {}
                for op in prog.ops[eng_name]:
                    need = {}
                    for d in op.waits:
                        if d.dma is not None:
                            key = ("dma", d.dma)
                            sem = dma_sems[d.dma]
                        else:
                            key = ("eng", d.eng)
                            sem = sems[d.eng]
                        if need.get(key, (None, 0))[1] < d.sigval:
                            need[key] = (sem, d.sigval)
                    for key, (sem, v) in need.items():
                        if known.get(key, 0) >= v:
                            continue
                        known[key] = v
                        eng.wait_ge(sem, v)
                    inst = op.fn(eng)
                    if op.dma is not None:
                        inst.then_inc(dma_sems[op.dma], 16)
                    elif op.signal:
                        inst.then_inc(sems[eng_name], 1)
                for key in final_waits.get(eng_name, ()):
                    eng.wait_ge(dma_sems[key], prog.dma_counts[key])
            return body

        block.tensor(run("pe"))
        block.scalar(run("act"))
        block.vector(run("dve"))
        block.gpsimd(run("pool"))
        block.sync(run("sp"))


def _fsz(ap):
    n = 1
    for d in ap.shape[1:]:
        n *= int(d)
    return n


def _with(fn, dur):
    fn.dur = dur
    return fn


def f_mm(out, lhsT, rhs, start, stop):
    return _with(lambda e: e.matmul(out, lhsT=lhsT, rhs=rhs, start=start, stop=stop), 0.02 + _fsz(out) / 1900.0)


def f_tr(out, in_, ident):
    return _with(lambda e: e.transpose(out, in_, ident), 0.1)


def f_act(out, in_, func, bias=None, scale=None, accum=None):
    kw = {}
    if bias is not None:
        kw["bias"] = bias
    if scale is not None:
        kw["scale"] = scale
    if accum is not None:
        kw["accum_out"] = accum
    return _with(lambda e: e.activation(out=out, in_=in_, func=func, **kw), 0.25 + _fsz(out) / 1200.0)


def f_tt(out, in0, in1, op):
    return _with(lambda e: e.tensor_tensor(out=out, in0=in0, in1=in1, op=op), 0.2 + _fsz(out) / 960.0)


def f_stt(out, in0, scalar, in1, op0, op1):
    return _with(lambda e: e.scalar_tensor_tensor(out=out, in0=in0, scalar=scalar, in1=in1, op0=op0, op1=op1),
                 0.2 + _fsz(out) / 960.0)


def f_ts(out, in0, s1, s2, op0, op1):
    return _with(lambda e: e.tensor_scalar(out=out, in0=in0, scalar1=s1, scalar2=s2, op0=op0, op1=op1),
                 0.2 + _fsz(out) / 960.0)


def f_ts1(out, in0, s1, op0):
    return _with(lambda e: e.tensor_single_scalar(out=out, in_=in0, scalar=s1, op=op0), 0.2 + _fsz(out) / 960.0)


def f_red(out, in_):
    return _with(lambda e: e.reduce_sum(out=out, in_=in_, axis=mybir.AxisListType.X), 0.2 + _fsz(in_) / 960.0)


def f_cp(out, in_):
    return _with(lambda e: e.tensor_copy(out=out, in_=in_), 0.2 + _fsz(out) / 960.0)


def f_rcp(out, in_):
    return _with(lambda e: e.reciprocal(out=out, in_=in_), 0.2 + _fsz(out) / 160.0)


def f_ms(out, val):
    return _with(lambda e: e.memset(out, val), 0.2 + _fsz(out) / 960.0)


def f_dma(out, in_):
    return _with(lambda e: e.dma_start(out=out, in_=in_), 2.0 + 128 * _fsz(out) * 3.0 / 200e3)


PIECES = (["xk0", "xk1", "xv0", "xv1", "b", "a", "q", "k", "v", "g", "o0", "o1",
           "xq0", "xq1", "xo0", "xo1"]
          + [n for jp in range(4) for n in ("u%d" % (2 * jp), "u%d" % (2 * jp + 1),
                                           "d%d" % (2 * jp), "d%d" % (2 * jp + 1))])
PID = {n: i for i, n in enumerate(PIECES)}
TILE_SEQ = PIECES[4:]


def build_program(NT):
    S = NT * T
    nc = bass.Bass("TRN2", target_bir_lowering=False)

    def din(name, shape, dt=F32):
        return nc.dram_tensor(name, shape, dt, kind="ExternalInput").ap()

    xT = din("xT", [D, S])
    memT = din("memT", [D, NMEM])
    w_in = din("w_in", [D, 3072])
    w_out = din("w_out", [D, D])
    w_xq = din("w_xq", [D, D])
    w_xk = din("w_xk", [D, D])
    w_xv = din("w_xv", [D, D])
    w_xo = din("w_xo", [D, D])
    w_up = din("w_up", [D, 4096])
    w_down = din("w_down", [4096, D])
    vec_d = din("vec", [128, NV])
    gnt_d = din("gnt", [128, 2, 512])
    qdt_d = din("qdt", [128, 4, 128])
    mask_d = din("maskt", [128, 4, 128])
    cos_d = din("cost", [128, S])
    sin_d = din("sint", [128, S])
    idn_d = din("idn", [128, 128])
    prm_d = din("prm", [128, 128])
    one_d = din("one", [128, 128])
    outT = nc.dram_tensor("outT", [D, S], F32, kind="ExternalOutput").ap()
    wscr = nc.dram_tensor("wscr", [len(PIECES), 128, 4096], BF16).ap()

    def wsrc(name):
        def cols(w, c0):
            return w[:, c0:c0 + 512].rearrange("(c p) n -> p c n", p=128)
        if name in ("b", "a", "q", "k", "v", "g"):
            c0 = {"a": 0, "b": 512, "q": 1024, "k": 1536, "v": 2048, "g": 2560}[name]
            return cols(w_in, c0)
        if name[0] == "o":
            return cols(w_out, 512 * int(name[1:]))
        if name[:2] == "xq":
            return cols(w_xq, 512 * int(name[2:]))
        if name[:2] == "xk":
            return cols(w_xk, 512 * int(name[2:]))
        if name[:2] == "xv":
            return cols(w_xv, 512 * int(name[2:]))
        if name[:2] == "xo":
            return cols(w_xo, 512 * int(name[2:]))
        if name[0] == "u":
            return cols(w_up, 512 * int(name[1:]))
        if name[0] == "d":
            r0 = 512 * int(name[1:])
            return w_down[r0:r0 + 512, :].rearrange("(c p) n -> p c n", p=128)
        raise KeyError(name)

    with ExitStack() as es:
        def sb(name, shape, dt):
            return es.enter_context(nc.sbuf_tensor(name, shape, dt))

        RING = [sb("ring%d" % i, [128, 8, 512], BF16) for i in range(4)]
        RA = sb("RA", [128, 8, 512], F32)
        XB = sb("XB", [128, 8, 512], BF16)
        RB = sb("RB", [128, 8, 512], BF16)
        CATS = [sb("CAT%d" % i, [128, 8, 512], BF16) for i in range(2)]
        H = sb("H", [128, 8, 512], BF16)
        ZB = [sb("ZB%d" % i, [128, 512], BF16) for i in range(4)]
        ZSQ = [sb("ZSQ%d" % i, [128, 512], BF16) for i in range(4)]
        U = sb("U", [128, 4, 542], F32)
        SGB = sb("SGB", [128, 4, 512], F32)
        QBT = [sb("QBT%d" % i, [128, 512], BF16) for i in range(2)]
        T1 = [sb("T1_%d" % i, [128, 512], F32) for i in range(2)]
        T2 = [sb("T2_%d" % i, [128, 512], F32) for i in range(2)]
        QP = sb("QP", [128, 4, 512], BF16)
        QD = sb("QD", [128, 4, 512], BF16)
        KP = sb("KP", [128, 4, 512], BF16)
        KD = [sb("KD%d" % i, [128, 4, 128], BF16) for i in range(2)]
        V = sb("V", [128, 4, 512], BF16)
        SG = sb("SG", [128, 4, 512], BF16)
        CF = sb("CF", [128, 4, 512], F32)
        STATS = {"b": [sb("%s_b" % n, [128, 512], F32) for n in ("MEAN", "MSQ")],
                 "f": [T1[0], T2[0]]}
        SKEY = {"b": (("MEAN", "b"), ("MSQ", "b")), "f": (("T1", 0), ("T2", 0))}
        for w in ("f", "b"):
            STATS[w].append(STATS[w][1])
        PT = [sb("PT%d" % i, [128, 128], BF16) for i in range(4)]
        ST = sb("ST", [128, 4, 128], F32)
        STB = sb("STB", [128, 4, 128], BF16)
        YN = [sb("YN0", [128, 512], F32)] * 2
        RT = [sb("RT%d" % i, [128, 512], BF16) for i in range(2)]
        GNS = [sb("GNS%d" % i, [128, 32], F32) for i in range(2)]
        PX = [sb("PX%d" % i, [128, 2, 512], BF16) for i in range(2)]
        RCP = [sb("RCP0", [128, 512], F32)] * 2
        RL = [sb("RL0", [128, 512], F32)]
        COS = sb("COS", [128, 512], F32)
        SIN = sb("SIN", [128, 512], F32)
        QDT = sb("QDT", [128, 4, 128], F32)
        MASKT = sb("MASKT", [128, 4, 128], F32)
        IDN = sb("IDN", [128, 128], BF16)
        PRM = sb("PRM", [128, 128], BF16)
        ONE = sb("ONE", [128, 128], BF16)
        VEC = sb("VEC", [128, NV], F32)
        GNT = sb("GNT", [128, 2, 512], F32)
        KXT = sb("KXT", [128, 8, 256], BF16)
        VX = sb("VX", [128, 2, 1024], BF16)

        PSF = [es.enter_context(nc.psum_tensor("psf%d" % i, [128, 512], F32)) for i in range(NPSF)]
        PSB = [es.enter_context(nc.psum_tensor("psb%d" % i, [128, 1024], BF16)) for i in range(8 - NPSF)]
        if len(PSB) == 1:
            PSB = [PSB[0], PSB[0]]

        sems = {e: es.enter_context(nc.semaphore("s_" + e)) for e in ("pe", "act", "dve", "pool")}
        CKEYS = ["c_idn", "c_prm", "c_one", "c_vec", "c_gnt", "c_qdt", "c_mask"]
        K_IDN, K_PRM, K_ONE, K_VEC, K_GNT, K_QDT, K_MASK = CKEYS
        dkeys = ([("ring", i) for i in range(4)] + [("wb", i) for i in range(len(PIECES))]
                 + CKEYS + ["mem", "xb", "cos", "sin"] + [("ra", c) for c in range(8)] + [("out", c) for c in range(8)])
        dsems = {k: es.enter_context(nc.semaphore("d%d" % i)) for i, k in enumerate(dkeys)}
        block = es.enter_context(nc.Block())

        def vcol(c):
            return VEC[:, c:c + 1]
        C_CONVB, C_CG, C_CB = 0, 4, 8
        C_LN = {1: (12, 20), 2: (28, 36), 3: (44, 52)}
        C_KDC, C_EPS, C_CW = 60, 64, 65

        def bcast(t, off, n):
            return bass.AP(t, off, [[32, 128], [1, 4], [0, n]])

        class Banks:
            def __init__(self):
                self.i = 0
                self.held = set()

            def next(self):
                while True:
                    b = self.i % NPSF
                    self.i += 1
                    if b not in self.held:
                        return PSF[b], ("psf", b)

            def hold(self):
                t, k = self.next()
                self.held.add(k[1])
                return t, k

            def unhold(self, k):
                self.held.discard(k[1])

        class RecWS:
            def __init__(self):
                self.seq = []

            def start(self):
                pass

            def get(self, name):
                self.seq.append(name)
                return 0

            def release(self, slot):
                pass

        class WStream:
            def __init__(self, P, seq):
                self.P = P
                self.seq = seq
                self.n_issued = 0
                self.n_got = 0
                self.free = [0, 1, 2, 3]
                self.where = {}

            def _issue(self):
                if self.n_issued >= len(self.seq) or not self.free:
                    return
                i = self.n_issued
                slot = self.free.pop(0)
                pid = PID[self.seq[i]]
                self.P.add("sp", f_dma(RING[slot][:], wscr[pid].rearrange("p (c n) -> p c n", c=8)),
                           reads=[("wb", pid)], writes=[("ring", slot)], dma=("ring", slot))
                self.where[i] = slot
                self.n_issued += 1

            def start(self):
                for _ in range(4):
                    self._issue()

            def get(self, name):
                i = self.n_got
                assert self.seq[i] == name, (self.seq[i], name)
                assert i in self.where, "ring exhausted"
                self.n_got += 1
                return self.where.pop(i)

            def release(self, slot):
                self.free.append(slot)
                self._issue()

        def emit_all(P, WS, picks, replay):
            PS = Banks()
            ri = {}

            def nxt(name, n):
                v = ri.get(name, 0)
                ri[name] = v + 1
                return v % n

            def load_xb(t):
                P.add("pool", f_dma(XB[:], xT[:, t * T:(t + 1) * T].rearrange("(c p) s -> p c s", p=128)),
                      writes=[("XB", c) for c in range(8)], dma="xb")

            def load_ra(t, c):
                P.add("sp", f_dma(RA[:, c, :], xT[c * 128:(c + 1) * 128, t * T:(t + 1) * T]),
                      writes=[("RA", c)], dma=("ra", c))

            def load_cs(t):
                P.add("pool", f_dma(COS[:], cos_d[:, t * T:(t + 1) * T]), writes=["COS"], dma="cos")
                P.add("pool", f_dma(SIN[:], sin_d[:, t * T:(t + 1) * T]), writes=["SIN"], dma="sin")

            for dst, src, key in ((IDN, idn_d, K_IDN), (PRM, prm_d, K_PRM), (ONE, one_d, K_ONE), (VEC, vec_d, K_VEC),
                                  (GNT, gnt_d, K_GNT), (QDT, qdt_d, K_QDT), (MASKT, mask_d, K_MASK)):
                P.add("pool", f_dma(dst[:], src), writes=[key], dma=key)
            MB = H[:, :, 0:256]
            P.add("pool", f_dma(MB, memT.rearrange("(c p) m -> p c m", p=128)), writes=["MB"], dma="mem")
            load_xb(0)
            load_cs(0)
            for c in range(8):
                load_ra(0, c)
            for name in PIECES:
                pid = PID[name]
                src = wsrc(name)
                c = src.shape[1]
                P.add("pool", f_dma(wscr[pid].rearrange("p (c n) -> p c n", c=c), src),
                      writes=[("wb", pid)], dma=("wb", pid))
            WS.start()
            P.add("dve", f_ms(U[:, :, 0:30], 0.0), writes=[("U", c) for c in range(4)])
            P.add("dve", f_ms(ST[:], 0.0), writes=["ST"])
            P.add("dve", f_ms(STB[:], 0.0), writes=["STB"])

            for nm in ("xk0", "xk1"):
                s_ = WS.get(nm)
                for o4 in range(4):
                    oc = 4 * int(nm[2]) + o4
                    bt, bk = PS.next()
                    for kc in range(8):
                        P.add("pe", f_mm(bt[:, 0:256], RING[s_][:, kc, o4 * 128:(o4 + 1) * 128], H[:, kc, 0:256],
                                         kc == 0, kc == 7),
                              reads=[("ring", s_), "MB"], writes=[bk])
                    P.add("act", f_act(KXT[:, oc, :], bt[:, 0:256], AF.Copy), writes=[bk, "KXT"])
                WS.release(s_)
            for half, nm in enumerate(("xv0", "xv1")):
                s_ = WS.get(nm)
                for mc in range(2):
                    bt, bk = PS.next()
                    for kc in range(8):
                        P.add("pe", f_mm(bt[:], H[:, kc, mc * 128:(mc + 1) * 128], RING[s_][:, kc, :], kc == 0, kc == 7),
                              reads=[("ring", s_), "MB"], writes=[bk])
                    P.add("act", f_act(VX[:, mc, half * 512:(half + 1) * 512], bt[:], AF.Copy), writes=[bk, "VX"])
                WS.release(s_)
            P.add("dve", f_ms(H[:, 0, 0:2], 0.0), writes=["MB"] + [("H", c) for c in range(8)])

            pend_stats = []

            def flush_stats(keep=0):
                while len(pend_stats) > keep:
                    pend_stats.pop(0)()

            def ln_stat_chunk(src, src_key, sum_b, sq_b, first, last, defer=0):
                i = nxt("z", 4)
                P.add("act", f_act(ZB[i][:], src, AF.Copy), reads=[src_key], writes=[("ZB", i)])
                P.add("act", f_act(ZSQ[i][:], src, AF.Square), reads=[src_key], writes=[("ZSQ", i)])

                def mms():
                    P.add("pe", f_mm(sum_b[0][:], ONE[:], ZB[i][:], first, last), reads=[("ZB", i), K_ONE],
                          writes=[sum_b[1]])
                    P.add("pe", f_mm(sq_b[0][:], ONE[:], ZSQ[i][:], first, last), reads=[("ZSQ", i), K_ONE],
                          writes=[sq_b[1]])
                pend_stats.append(mms)
                flush_stats(defer)

            def ln_post(w, sum_b, sq_b, dd):
                MEAN, MSQ, RSTD = STATS[w]
                km, kq = SKEY[w]
                kr = kq
                inv = 1.0 / dd
                P.add("act", f_act(MEAN[:], sum_b[0][:], AF.Copy, scale=inv), writes=[sum_b[1], km])
                P.add("dve", f_tt(MSQ[:], MEAN[:], MEAN[:], ALU.mult), reads=[km], writes=[kq])
                P.add("dve", f_stt(MSQ[:], sq_b[0][:], inv, MSQ[:], ALU.mult, ALU.subtract),
                      writes=[sq_b[1], kq])
                P.add("act", f_act(RSTD[:], MSQ[:], AF.Ln, bias=vcol(C_EPS), scale=1.0),
                      reads=[kq, K_VEC], writes=[kr])
                P.add("act", f_act(RSTD[:], RSTD[:], AF.Exp, scale=-0.5), writes=[kr])
                PS.unhold(sq_b[1])

            def ln_norm_chunk(which, oc, make_rb, sum_b):
                MEAN, MSQ, RSTD = STATS["b"]
                cg, cb = C_LN[which]
                P.add("dve", f_stt(RA[:, oc, :], sum_b[0][:], -1.0 / D, RA[:, oc, :], ALU.mult, ALU.add),
                      writes=[sum_b[1], ("RA", oc)])
                P.add("pool", f_tt(RA[:, oc, :], RA[:, oc, :], RSTD[:], ALU.mult),
                      reads=[("MSQ", "b")], writes=[("RA", oc)])
                if make_rb:
                    P.add("act", f_act(RB[:, oc, :], RA[:, oc, :], AF.Identity, bias=vcol(cb + oc), scale=vcol(cg + oc)),
                          reads=[K_VEC, ("RA", oc)], writes=[("RB", oc)])
                P.add("act", f_act(RA[:, oc, :], RA[:, oc, :], AF.Identity, bias=vcol(cb + oc), scale=vcol(cg + oc)),
                      reads=[K_VEC], writes=[("RA", oc)])

            def residual_ln(which, pieces, rhs_buf, rhs_key, make_rb):
                sum_b = PS.hold()
                sq_b = PS.hold()
                s_ = None
                for oc in range(8):
                    if oc % 4 == 0:
                        s_ = WS.get(pieces[oc // 4])
                    bt, bk = PS.next()
                    for kc in range(8):
                        P.add("pe", f_mm(bt[:], RING[s_][:, kc, (oc % 4) * 128:(oc % 4 + 1) * 128], rhs_buf[:, kc, :],
                                         kc == 0, kc == 7),
                              reads=[("ring", s_), (rhs_key, kc)], writes=[bk])
                    P.add("dve", f_stt(RA[:, oc, :], RA[:, oc, :], ALPHA, bt[:], ALU.mult, ALU.add),
                          writes=[bk, ("RA", oc)])
                    ln_stat_chunk(RA[:, oc, :], ("RA", oc), sum_b, sq_b, oc == 0, oc == 7, defer=2)
                    if oc % 4 == 3:
                        WS.release(s_)
                    yield
                flush_stats()
                ln_post("b", sum_b, sq_b, float(D))
                for oc in range(8):
                    ln_norm_chunk(which, oc, make_rb, sum_b)
                    if oc % 2 == 1:
                        yield
                PS.unhold(sum_b[1])

            convq = []
            cat_free = [-1]

            def front(t):
                P.tile["f"] = t
                CAT = CATS[t % 2]
                ck = "CAT%d" % (t % 2)
                P.ph["f"] = "glu"
                s_ = WS.get("b")
                for cc in range(4):
                    b1, k1 = PS.next()
                    for kc in range(8):
                        P.add("pe", f_mm(b1[:], RING[s_][:, kc, cc * 128:(cc + 1) * 128], XB[:, kc, :], kc == 0, kc == 7),
                              reads=[("ring", s_), ("XB", kc)], writes=[k1])
                    P.add("act", f_act(SGB[:, cc, :], b1[:], AF.Sigmoid), writes=[k1, ("SGB", cc)])
                    yield
                WS.release(s_)
                s_ = WS.get("a")
                for cc in range(4):
                    b2, k2 = PS.next()
                    for kc in range(8):
                        P.add("pe", f_mm(b2[:], RING[s_][:, kc, cc * 128:(cc + 1) * 128], XB[:, kc, :], kc == 0, kc == 7),
                              reads=[("ring", s_), ("XB", kc)], writes=[k2])
                    P.add("dve", f_tt(U[:, cc, 30:542], b2[:], SGB[:, cc, :], ALU.mult),
                          reads=[("SGB", cc)], writes=[k2, ("U", cc)])
                    yield
                WS.release(s_)

                for cc in range(4):
                    convq.append(lambda cc=cc: P.add(
                        "pool", f_ts(CF[:, cc, :], U[:, cc, 0:512], vcol(C_CW + cc * 31), vcol(C_CONVB + cc),
                                     ALU.mult, ALU.add),
                        reads=[("U", cc), K_VEC], writes=[("CF", cc)]))
                    for k in range(1, 31):
                        convq.append(lambda cc=cc, k=k: P.add(
                            "dve", _with(f_stt(CF[:, cc, :], U[:, cc, k:k + 512], vcol(C_CW + cc * 31 + k), CF[:, cc, :],
                                               ALU.mult, ALU.add), 0.73 * CONVSCALE),
                            reads=[("U", cc)], writes=[("CF", cc)]))
                    convq.append(lambda cc=cc: P.add("pool", f_cp(U[:, cc, 0:30], U[:, cc, 512:542]),
                                                     writes=[("U", cc)]))

                def drain(n):
                    cur = P.cur
                    for _ in range(min(n, len(convq))):
                        convq.pop(0)()
                    P.cur = cur

                P.ph["f"] = "qk"
                pend_rot = []
                for nm in ("q", "k"):
                    s_ = WS.get(nm)
                    sc = 128.0 ** -0.5 if nm == "q" else 1.0
                    dest, dkey = (QP, "QP") if nm == "q" else (KP, "KP")
                    for h in range(4):
                        b1, k1 = PS.hold()
                        for kc in range(8):
                            P.add("pe", f_mm(b1[:], RING[s_][:, kc, h * 128:(h + 1) * 128], XB[:, kc, :], kc == 0, kc == 7),
                                  reads=[("ring", s_), ("XB", kc)], writes=[k1])
                        i = nxt("rot", 2)
                        P.add("act", f_act(QBT[i][:], b1[:], AF.Copy, scale=sc), writes=[k1, ("QBT", i)])

                        def rot(b1=b1, k1=k1, i=i, sc=sc, dest=dest, dkey=dkey, h=h, nm=nm):
                            b2, k2 = PS.next()
                            P.add("pe", f_mm(b2[:], PRM[:], QBT[i][:], True, True), reads=[("QBT", i), K_PRM], writes=[k2])
                            P.add("dve", f_stt(T1[i][:], b1[:], sc, COS[:], ALU.mult, ALU.mult),
                                  reads=["COS"], writes=[k1, ("T1", i)])
                            PS.unhold(k1)
                            P.add("dve", f_tt(T2[i][:], b2[:], SIN[:], ALU.mult), reads=["SIN"], writes=[k2, ("T2", i)])
                            P.add("pool", f_tt(dest[:, h, :], T1[i][:], T2[i][:], ALU.add),
                                  reads=[("T1", i), ("T2", i)], writes=[(dkey, h)])
                            if nm == "q":
                                for j in range(4):
                                    P.add("pool", f_tt(QD[:, h, j * 128:(j + 1) * 128], QP[:, h, j * 128:(j + 1) * 128],
                                                       QDT[:, h, :], ALU.mult),
                                          reads=[("QP", h), K_QDT], writes=[("QD", h)])
                        pend_rot.append(rot)
                        while len(pend_rot) > 1:
                            pend_rot.pop(0)()
                        drain(FD_QK)
                        yield
                    WS.release(s_)
                while pend_rot:
                    pend_rot.pop(0)()
                yield
                if t + 1 < NT:
                    load_cs(t + 1)

                P.ph["f"] = "vg"
                for nm in ("v", "g"):
                    s_ = WS.get(nm)
                    for j in range(4):
                        b1, k1 = PS.next()
                        for kc in range(8):
                            P.add("pe", f_mm(b1[:], XB[:, kc, j * 128:(j + 1) * 128], RING[s_][:, kc, :], kc == 0, kc == 7),
                                  reads=[("ring", s_), ("XB", kc)], writes=[k1])
                        if nm == "v":
                            P.add("act", f_act(V[:, j, :], b1[:], AF.Copy), writes=[k1, ("V", j)])
                        else:
                            P.add("act", f_act(SG[:, j, :], b1[:], AF.Silu), writes=[k1, ("SG", j)])
                        drain(FD_VG)
                        yield
                    WS.release(s_)
                if t + 1 < NT:
                    load_xb(t + 1)

                P.ph["f"] = "ret"
                pend_tail = []
                for j in range(4):
                    js = slice(j * 128, (j + 1) * 128)
                    bs, ks = PS.hold()
                    for h in range(4):
                        P.add("pe", f_mm(bs[:, h * 128:(h + 1) * 128], KP[:, h, js], QP[:, h, js], True, True),
                              reads=[("KP", h), ("QP", h)], writes=[ks])
                    for h in range(4):
                        P.add("pe", f_tr(PSB[0][:, h * 128:(h + 1) * 128], KP[:, h, js], IDN[:]),
                              reads=[("KP", h), K_IDN], writes=[("psb", 0)])
                    kdi = nxt("kd", 2)
                    for h in range(4):
                        P.add("act", f_act(KD[kdi][:, h, :], PSB[0][:, h * 128:(h + 1) * 128], AF.Copy,
                                           scale=vcol(C_KDC + h)),
                              reads=[K_VEC], writes=[("psb", 0), ("KD", kdi)])
                    drain(FD_RET)
                    yield
                    by, ky = PS.hold()
                    for h in range(4):
                        pi = nxt("pt", 4)
                        P.add("dve", f_tt(PT[pi][:], bs[:, h * 128:(h + 1) * 128], MASKT[:, h, :], ALU.mult),
                              reads=[K_MASK], writes=[ks, ("PT", pi)])
                        P.add("pe", f_mm(by[:, h * 128:(h + 1) * 128], PT[pi][:], V[:, j, h * 128:(h + 1) * 128],
                                         True, False),
                              reads=[("PT", pi), ("V", j)], writes=[ky])
                        P.add("pe", f_mm(by[:, h * 128:(h + 1) * 128], QD[:, h, js], STB[:, h, :], False, True),
                              reads=[("QD", h), "STB"], writes=[ky])
                    PS.unhold(ks)
                    drain(FD_RET)
                    yield
                    bu, ku = PS.next()
                    for h in range(4):
                        P.add("pe", f_mm(bu[:, h * 128:(h + 1) * 128], KD[kdi][:, h, :], V[:, j, h * 128:(h + 1) * 128],
                                         True, True),
                              reads=[("KD", kdi), ("V", j)], writes=[ku])
                    for h in range(4):
                        P.add("dve", f_stt(ST[:, h, :], ST[:, h, :], GAM[h] ** 128, bu[:, h * 128:(h + 1) * 128],
                                           ALU.mult, ALU.add),
                              writes=[ku, "ST"])
                    P.add("act", f_act(STB[:], ST[:], AF.Copy), reads=["ST"], writes=["STB"])
                    while pend_tail:
                        pend_tail.pop(0)()
                    drain(FD_RET)
                    yield
                    yi = nxt("yn", 2)
                    g = GNS[yi]
                    gk = ("GNS", yi)
                    by3 = by[:].rearrange("p (h e) -> p h e", h=4)
                    yn3 = YN[yi][:].rearrange("p (h e) -> p h e", h=4)
                    P.add("act", f_act(YN[yi][:], by[:], AF.Square), writes=[ky, ("YN", 0)])
                    P.add("dve", f_red(g[:, 0:4], by3), writes=[ky, gk])
                    P.add("dve", f_red(g[:, 4:8], yn3), reads=[("YN", 0)], writes=[gk])
                    P.add("dve", f_ts1(g[:, 8:12], g[:, 0:4], 1.0 / 128, ALU.mult), writes=[gk])
                    P.add("dve", f_tt(g[:, 12:16], g[:, 8:12], g[:, 8:12], ALU.mult), writes=[gk])
                    P.add("dve", f_stt(g[:, 16:20], g[:, 4:8], 1.0 / 128, g[:, 12:16], ALU.mult, ALU.subtract),
                          writes=[gk])
                    P.add("act", f_act(g[:, 20:24], g[:, 16:20], AF.Sqrt, bias=vcol(C_EPS), scale=1.0),
                          reads=[K_VEC], writes=[gk])
                    P.add("dve", f_rcp(g[:, 20:24], g[:, 20:24]), writes=[gk])
                    P.add("dve", f_tt(yn3, by3, bcast(g, 8, 128), ALU.subtract), reads=[gk], writes=[ky, ("YN", 0)])
                    P.add("dve", f_tt(yn3, yn3, bcast(g, 20, 128), ALU.mult), reads=[gk], writes=[("YN", 0)])
                    P.add("pool", f_tt(YN[yi][:], YN[yi][:], GNT[:, 0, :], ALU.mult), reads=[K_GNT], writes=[("YN", 0)])
                    P.add("pool", f_tt(YN[yi][:], YN[yi][:], GNT[:, 1, :], ALU.add), reads=[K_GNT], writes=[("YN", 0)])
                    P.add("pool", f_tt(RT[yi][:], YN[yi][:], SG[:, j, :], ALU.mult), reads=[("YN", 0), ("SG", j)],
                          writes=[("RT", yi)])
                    def tail(yi=yi, js=js):
                        for h in range(4):
                            P.add("pe", f_tr(PSB[1][:, h * 128:(h + 1) * 128], RT[yi][:, h * 128:(h + 1) * 128], IDN[:]),
                                  reads=[("RT", yi), K_IDN], writes=[("psb", PSB1K)])
                        P.add("act", f_act(CAT[:, 4:8, js], PSB[1][:, 0:512].rearrange("p (h n) -> p h n", h=4), AF.Copy),
                              writes=[("psb", PSB1K)] + [(ck, 4 + h) for h in range(4)])
                    pend_tail.append(tail)
                    PS.unhold(ky)
                    drain(FD_RET)
                    yield
                while pend_tail:
                    pend_tail.pop(0)()
                drain(len(convq))
                yield
                P.ph["f"] = "convln"
                sum_b = PS.hold()
                sq_b = PS.hold()
                for cc in range(4):
                    ln_stat_chunk(CF[:, cc, :], ("CF", cc), sum_b, sq_b, cc == 0, cc == 3)
                ln_post("f", sum_b, sq_b, 512.0)
                MEAN, MSQ, RSTD = STATS["f"]
                for cc in range(4):
                    P.add("dve", f_stt(CF[:, cc, :], sum_b[0][:], -1.0 / 512.0, CF[:, cc, :], ALU.mult, ALU.add),
                          writes=[sum_b[1], ("CF", cc)])
                    P.add("pool", f_tt(CF[:, cc, :], CF[:, cc, :], RSTD[:], ALU.mult), reads=[SKEY["f"][1]],
                          writes=[("CF", cc)])
                    P.add("act", f_act(CAT[:, cc, :], CF[:, cc, :], AF.Silu, bias=vcol(C_CB + cc), scale=vcol(C_CG + cc)),
                          reads=[("CF", cc), K_VEC], writes=[(ck, cc)])
                PS.unhold(sum_b[1])
                yield

            def back(t):
                P.tile["b"] = t
                CAT = CATS[t % 2]
                ck = "CAT%d" % (t % 2)
                P.ph["b"] = "wout"
                yield from residual_ln(1, ("o0", "o1"), CAT, ck, True)
                P.ph["b"] = "xattn"
                s_ = None
                for oc in range(8):
                    if oc % 4 == 0:
                        s_ = WS.get("xq%d" % (oc // 4))
                    b1, k1 = PS.next()
                    for kc in range(8):
                        P.add("pe", f_mm(b1[:], RING[s_][:, kc, (oc % 4) * 128:(oc % 4 + 1) * 128], RB[:, kc, :],
                                         kc == 0, kc == 7),
                              reads=[("ring", s_), ("RB", kc)], writes=[k1])
                    P.add("act", f_act(CAT[:, oc, :], b1[:], AF.Copy), writes=[k1, (ck, oc)])
                    if oc % 4 == 3:
                        WS.release(s_)
                    yield
                for h in range(4):
                    pi = nxt("px", 2)
                    for mc in range(2):
                        b1, k1 = PS.next()
                        for dc in range(2):
                            P.add("pe", f_mm(b1[:], KXT[:, 2 * h + dc, mc * 128:(mc + 1) * 128], CAT[:, 2 * h + dc, :],
                                             dc == 0, dc == 1),
                                  reads=["KXT", (ck, 2 * h + dc)], writes=[k1])
                        P.add("act", f_act(PX[pi][:, mc, :], b1[:], AF.Exp, scale=1.0 / 16.0), writes=[k1, ("PX", pi)])
                    yield
                    b1, k1 = PS.next()
                    for mc in range(2):
                        P.add("pe", f_mm(b1[:], ONE[:], PX[pi][:, mc, :], mc == 0, mc == 1),
                              reads=[("PX", pi), K_ONE], writes=[k1])
                    P.add("act", f_act(RCP[pi][:], b1[:], AF.Ln), writes=[k1, ("RCP", 0)])
                    P.add("act", f_act(RCP[pi][:], RCP[pi][:], AF.Exp, scale=-1.0), writes=[("RCP", 0)])
                    for dc in range(2):
                        b2, k2 = PS.next()
                        for mc in range(2):
                            P.add("pe", f_mm(b2[:], VX[:, mc, (2 * h + dc) * 128:(2 * h + dc + 1) * 128],
                                             PX[pi][:, mc, :], mc == 0, mc == 1),
                                  reads=["VX", ("PX", pi)], writes=[k2])
                        P.add("dve", f_tt(CAT[:, 2 * h + dc, :], b2[:], RCP[pi][:], ALU.mult),
                              reads=[("RCP", 0)], writes=[k2, (ck, 2 * h + dc)])
                    yield
                P.ph["b"] = "wxo"
                yield from residual_ln(2, ("xo0", "xo1"), CAT, ck, True)
                cat_free[0] = t
                P.ph["b"] = "mlp"

                for jp in range(4):
                    if jp == 3:
                        sum_b = PS.hold()
                        sq_b = PS.hold()
                    s_ = None
                    for hc in range(8):
                        if hc % 4 == 0:
                            s_ = WS.get("u%d" % (2 * jp + hc // 4))
                        b1, k1 = PS.next()
                        for kc in range(8):
                            P.add("pe", f_mm(b1[:], RING[s_][:, kc, (hc % 4) * 128:(hc % 4 + 1) * 128], RB[:, kc, :],
                                             kc == 0, kc == 7),
                                  reads=[("ring", s_), ("RB", kc)], writes=[k1])
                        i = 0
                        P.add("act", f_act(RL[i][:], b1[:], AF.Relu), writes=[k1, ("RL", i)])
                        P.add("act", f_act(H[:, hc, :], RL[i][:], AF.Square), reads=[("RL", i)], writes=[("H", hc)])
                        if hc % 4 == 3:
                            WS.release(s_)
                        yield
                    sd = [WS.get("d%d" % (2 * jp)), WS.get("d%d" % (2 * jp + 1))]
                    for oc in range(8):
                        b1, k1 = PS.next()
                        for kc8 in range(8):
                            s_ = sd[kc8 // 4]
                            kk = kc8 % 4
                            P.add("pe", f_mm(b1[:], RING[s_][:, kk * 2 + oc // 4, (oc % 4) * 128:(oc % 4 + 1) * 128],
                                             H[:, kc8, :], kc8 == 0, kc8 == 7),
                                  reads=[("ring", s_), ("H", kc8)], writes=[k1])
                        if jp == 0:
                            P.add("dve", f_stt(RA[:, oc, :], RA[:, oc, :], ALPHA, b1[:], ALU.mult, ALU.add),
                                  writes=[k1, ("RA", oc)])
                        else:
                            P.add("dve", f_tt(RA[:, oc, :], RA[:, oc, :], b1[:], ALU.add), writes=[k1, ("RA", oc)])
                        if jp == 3:
                            ln_stat_chunk(RA[:, oc, :], ("RA", oc), sum_b, sq_b, oc == 0, oc == 7, defer=2)
                        yield
                    WS.release(sd[0])
                    WS.release(sd[1])
                flush_stats()
                ln_post("b", sum_b, sq_b, float(D))
                for oc in range(8):
                    ln_norm_chunk(3, oc, False, sum_b)
                    P.add("sp", f_dma(outT[oc * 128:(oc + 1) * 128, t * T:(t + 1) * T], RA[:, oc, :]),
                          reads=[("RA", oc)], dma=("out", oc))
                    if oc >= 1 and t + 1 < NT:
                        load_ra(t + 1, oc - 1)
                    if oc % 2 == 1:
                        yield
                PS.unhold(sum_b[1])
                if t + 1 < NT:
                    load_ra(t + 1, 7)

            def chain_front():
                for t in range(NT):
                    while t >= 2 and (back_done[0] if NOAHEAD else cat_free[0]) < t - 2:
                        yield "blocked"
                    yield from front(t)
                    front_done[0] = t

            def chain_back():
                for t in range(NT):
                    while front_done[0] < t:
                        yield "blocked"
                    yield from back(t)
                    back_done[0] = t

            front_done = [-1]
            back_done = [-1]
            gf = chain_front()
            gb_ = chain_back()
            tf = tb = 0.0
            alive_f = alive_b = True
            blocked_f = blocked_b = False
            while alive_f or alive_b:
                can_b = alive_b and not blocked_b
                can_f = alive_f and not blocked_f
                if can_b and can_f:
                    pick_b = tb <= tf + BIAS
                elif can_b or can_f:
                    pick_b = can_b
                else:
                    blocked_f = blocked_b = False
                    continue
                if replay is not None:
                    pick_b = replay.pop(0)
                else:
                    picks.append(pick_b)
                P.mark = 0.0
                P.first = None
                P.cur = "b" if pick_b else "f"
                try:
                    r = next(gb_ if pick_b else gf)
                except StopIteration:
                    if pick_b:
                        alive_b = False
                    else:
                        alive_f = False
                    blocked_f = blocked_b = False
                    continue
                if r == "blocked":
                    if pick_b:
                        blocked_b = True
                    else:
                        blocked_f = True
                    continue
                blocked_f = blocked_b = False
                if pick_b:
                    tb = max(tb, P.mark)
                else:
                    tf = max(tf, P.mark)
            P.sim_total = max(P.free.values())

        FD_QK, FD_VG, FD_RET, BD = 6, 5, 2, 0
        BIAS = 5.0
        NOAHEAD = False
        CONVSCALE = 1.0
        rec = RecWS()
        picks = []
        emit_all(Prog(), rec, picks, None)
        P = Prog()
        P.dump = False
        emit_all(P, WStream(P, rec.seq), None, list(picks))
        P.emit(block, sems, dsems, {"sp": [("out", c) for c in range(8)]})
    return nc


def _constants(S):
    pos = np.arange(S, dtype=np.float32)
    inv = (np.float32(10000.0) ** (-np.linspace(0.0, 1.0, 64, dtype=np.float32))).astype(np.float32)
    ang = (pos[:, None] * inv[None, :]).astype(np.float32)
    cos = np.cos(ang).astype(np.float32).T
    sin = np.sin(ang).astype(np.float32).T
    cost = np.concatenate([cos, cos], axis=0)
    sint = np.concatenate([-sin, sin], axis=0)
    gam = np.array(GAM, dtype=np.float64)
    n = np.arange(128, dtype=np.float64)
    qdt = np.broadcast_to((gam[:, None] ** (n[None, :] + 1.0))[None], (128, 4, 128)).astype(np.float32)
    m = n[:, None]
    nn = n[None, :]
    maskt = np.stack([np.where(m <= nn, g ** (nn - m), 0.0) for g in gam], axis=1).astype(np.float32)
    kdc = np.stack([g ** (127.0 - n) for g in gam], axis=1).astype(np.float32)
    idn = np.eye(128, dtype=np.float32)
    prm = np.roll(idn, 64, axis=1).copy()
    one = np.ones((128, 128), dtype=np.float32)
    return (np.ascontiguousarray(cost), np.ascontiguousarray(sint), np.ascontiguousarray(qdt),
            np.ascontiguousarray(maskt), kdc, idn, prm, one)


def _fm(v, nchunk):
    return np.asarray(v, dtype=np.float32).reshape(nchunk, 128).T


def _prepare(inputs, S):
    cost, sint, qdt, maskt, kdc, idn, prm, one = _constants(S)
    vec = np.zeros((128, NV), dtype=np.float32)
    vec[:, 0:4] = _fm(inputs["conv_b"][0], 4)
    vec[:, 4:8] = _fm(inputs["conv_ln_g"][0], 4)
    vec[:, 8:12] = _fm(inputs["conv_ln_b"][0], 4)
    for i, nm in enumerate(("ln1_g", "ln1_b", "ln2_g", "ln2_b", "ln3_g", "ln3_b")):
        vec[:, 12 + 8 * i:20 + 8 * i] = _fm(inputs[nm][0], 8)
    vec[:, 60:64] = kdc
    vec[:, 64] = EPS
    cw = np.asarray(inputs["conv_w"][0], dtype=np.float32)
    vec[:, 65:65 + 124] = cw.T.reshape(4, 128, 31).transpose(1, 0, 2).reshape(128, 124)
    gnt = np.stack([np.broadcast_to(np.asarray(inputs["ret_gn_g"][0], np.float32)[None, :], (128, 512)),
                    np.broadcast_to(np.asarray(inputs["ret_gn_b"][0], np.float32)[None, :], (128, 512))], axis=1)
    shared = {
        "w_in": np.ascontiguousarray(inputs["w_in"][0], dtype=np.float32),
        "w_out": np.ascontiguousarray(inputs["w_out"][0], dtype=np.float32),
        "w_xq": np.ascontiguousarray(inputs["w_xq"][0], dtype=np.float32),
        "w_xk": np.ascontiguousarray(inputs["w_xk"][0], dtype=np.float32),
        "w_xv": np.ascontiguousarray(inputs["w_xv"][0], dtype=np.float32),
        "w_xo": np.ascontiguousarray(inputs["w_xo"][0], dtype=np.float32),
        "w_up": np.ascontiguousarray(inputs["w_up"][0], dtype=np.float32),
        "w_down": np.ascontiguousarray(inputs["w_down"][0], dtype=np.float32),
        "vec": vec, "gnt": np.ascontiguousarray(gnt), "qdt": qdt, "maskt": maskt,
        "cost": cost, "sint": sint, "idn": idn, "prm": prm, "one": one,
    }
    return shared


def run(inputs, S, n_cores):
    shared = _prepare(inputs, S)
    x = np.asarray(inputs["x"], dtype=np.float32)
    mem = np.asarray(inputs["mem"], dtype=np.float32)
    in_maps = []
    for b in range(n_cores):
        m = dict(shared)
        m["xT"] = np.ascontiguousarray(x[b, :S].T)
        m["memT"] = np.ascontiguousarray(mem[b].T)
        in_maps.append(m)
    nc = build_program(S // T)
    res = run_bass_kernel_spmd(nc, in_maps, core_ids=list(range(n_cores)))
    out = np.stack([np.ascontiguousarray(res.results[b]["outT"].T) for b in range(n_cores)], axis=0)
    return out.astype(np.float32)


def kernel(**inputs):
    return run(inputs, SEQ, NB)
```

```python
from contextlib import ExitStack

import numpy as np
import concourse.bass as bass
import concourse.mybir as mybir
from concourse.bass_utils import run_bass_kernel_spmd

F32 = mybir.dt.float32
BF16 = mybir.dt.bfloat16
AF = mybir.ActivationFunctionType
ALU = mybir.AluOpType

D = 1024
SEQ = 4096
NB = 8
NMEM = 256
T = 512
ALPHA = 2.0 ** 0.25
EPS = 1e-5
NV = 192
GAM = [1.0 - 2.0 ** (-5.0 - h) for h in range(4)]

ENGS = ("pe", "act", "dve", "pool", "sp")

ENG_SCALE = {}
NPSF = 7
PSB1K = 1 if NPSF == 6 else 0
LNMODE = 0


class _Op:
    __slots__ = ("eng", "fn", "signal", "waits", "dma", "sigval", "fin", "ph", "idx")

    def __init__(self, eng, fn, dma):
        self.eng = eng
        self.fn = fn
        self.dma = dma
        self.signal = False
        self.waits = []
        self.sigval = None
        self.fin = 0.0


class Prog:
    def __init__(self):
        self.ops = {e: [] for e in ENGS}
        self.last_w = {}
        self.readers = {}
        self.dma_counts = {}
        self.free = {e: 0.0 for e in ENGS}
        self.ph = {"f": "pro", "b": "pro"}
        self.tile = {"f": 0, "b": 0}
        self.spans = {}
        self.cur = "f"
        self.gaps = {}
        self.busy = {e: 0.0 for e in ENGS}
        self.mark = 0.0
        self.first = None

    def add(self, eng, fn, reads=(), writes=(), dma=None):
        op = _Op(eng, fn, dma)
        op.idx = len(self.ops[eng])
        if dma is not None:
            self.dma_counts[dma] = self.dma_counts.get(dma, 0) + 16
            op.sigval = self.dma_counts[dma]
        deps = []
        for k in reads:
            w = self.last_w.get(k)
            if w is not None:
                deps.append((w, k))
        for k in writes:
            w = self.last_w.get(k)
            if w is not None:
                deps.append((w, k))
            deps.extend((r, k) for r in self.readers.get(k, {}).values())
        t0 = self.free[eng]
        op.ph = self.ph[self.cur]
        crit = None
        for d, k in deps:
            lat = 0.0 if (d.eng == eng and d.dma is None) else 0.06
            if d.fin + lat > t0:
                t0 = d.fin + lat
                crit = d
        if crit is not None and t0 > self.free[eng]:
            gk = (eng, op.ph, crit.eng if crit.dma is None else "dma", crit.ph)
            self.gaps[gk] = self.gaps.get(gk, 0.0) + t0 - self.free[eng]
        best = {}
        for d, k in deps:
            if d is op:
                continue
            if d.dma is None and d.eng == eng:
                if eng == "pe":
                    continue
                if isinstance(k, tuple) and k[0] in ("psf", "psb"):
                    continue
            g = ("dma", d.dma) if d.dma is not None else ("eng", d.eng)
            cur = best.get(g)
            if cur is None or (d.sigval > cur.sigval if d.dma is not None else d.idx > cur.idx):
                best[g] = d
        for d in best.values():
            op.waits.append(d)
            d.signal = True
        g_self = ("dma", dma) if dma is not None else ("eng", eng)
        for k in reads:
            self.readers.setdefault(k, {})[g_self] = op
        for k in writes:
            self.last_w[k] = op
            self.readers[k] = {}
        dur = getattr(fn, "dur", 0.5)
        if eng == "pool" and dma is None:
            dur = 2.0 * dur
        if dma is None:
            dur *= ENG_SCALE.get(eng, 1.0)
        op.fin = t0 + dur
        self.busy[eng] += dur
        if self.cur == "b" and self.tile["b"] == 3 and op.ph == "wout" and getattr(self, "dump", False) and eng != "sp":
            print("    OP %-4s t0=%8.2f fin=%8.2f dur=%5.2f qwait=%6.2f crit=%s" % (
                eng, t0, op.fin, dur, t0 - self.free[eng] if False else 0.0,
                (crit.eng + "/" + crit.ph + "@%.1f" % crit.fin) if crit is not None else "-"))
        sk = (self.cur, self.tile[self.cur], op.ph)
        sp = self.spans.get(sk)
        if sp is None:
            self.spans[sk] = [t0, op.fin, 0.0]
            sp = self.spans[sk]
        sp[1] = max(sp[1], op.fin)
        if eng == "pe":
            sp[2] += dur
        self.free[eng] = t0 + (0.1 if dma is not None else dur)
        if op.fin > self.mark:
            self.mark = op.fin
        if self.first is None:
            self.first = op.fin
        self.ops[eng].append(op)
        return op

    def emit(self, block, sems, dma_sems, final_waits):
        for e in ENGS:
            c = 0
            for op in self.ops[e]:
                if op.dma is None and op.signal:
                    c += 1
                    op.sigval = c
            self.sigcount = getattr(self, "sigcount", {})
            self.sigcount[e] = (c, len(self.ops[e]))
        prog = self

        def run(eng_name):
            def body(eng):
                known = {}
                for op in prog.ops[eng_name]:
                    need = {}
                    for d in op.waits:
                        if d.dma is not None:
                            key = ("dma", d.dma)
                            sem = dma_sems[d.dma]
                        else:
                            key = ("eng", d.eng)
                            sem = sems[d.eng]
                        if need.get(key, (None, 0))[1] < d.sigval:
                            need[key] = (sem, d.sigval)
                    for key, (sem, v) in need.items():
                        if known.get(key, 0) >= v:
                            continue
                        known[key] = v
                        eng.wait_ge(sem, v)
                    inst = op.fn(eng)
                    if op.dma is not None:
                        inst.then_inc(dma_sems[op.dma], 16)
                    elif op.signal:
                        inst.then_inc(sems[eng_name], 1)
                for key in final_waits.get(eng_name, ()):
                    eng.wait_ge(dma_sems[key], prog.dma_counts[key])
            return body

        block.tensor(run("pe"))
        block.scalar(run("act"))
        block.vector(run("dve"))
        block.gpsimd(run("pool"))
        block.sync(run("sp"))


def _fsz(ap):
    n = 1
    for d in ap.shape[1:]:
        n *= int(d)
    return n


def _with(fn, dur):
    fn.dur = dur
    return fn


def f_mm(out, lhsT, rhs, start, stop):
    return _with(lambda e: e.matmul(out, lhsT=lhsT, rhs=rhs, start=start, stop=stop), 0.02 + _fsz(out) / 1900.0)


def f_tr(out, in_, ident):
    return _with(lambda e: e.transpose(out, in_, ident), 0.1)


def f_act(out, in_, func, bias=None, scale=None, accum=None):
    kw = {}
    if bias is not None:
        kw["bias"] = bias
    if scale is not None:
        kw["scale"] = scale
    if accum is not None:
        kw["accum_out"] = accum
    return _with(lambda e: e.activation(out=out, in_=in_, func=func, **kw), 0.25 + _fsz(out) / 1200.0)


def f_tt(out, in0, in1, op):
    return _with(lambda e: e.tensor_tensor(out=out, in0=in0, in1=in1, op=op), 0.2 + _fsz(out) / 960.0)


def f_stt(out, in0, scalar, in1, op0, op1):
    return _with(lambda e: e.scalar_tensor_tensor(out=out, in0=in0, scalar=scalar, in1=in1, op0=op0, op1=op1),
                 0.2 + _fsz(out) / 960.0)


def f_ts(out, in0, s1, s2, op0, op1):
    return _with(lambda e: e.tensor_scalar(out=out, in0=in0, scalar1=s1, scalar2=s2, op0=op0, op1=op1),
                 0.2 + _fsz(out) / 960.0)


def f_ts1(out, in0, s1, op0):
    return _with(lambda e: e.tensor_single_scalar(out=out, in_=in0, scalar=s1, op=op0), 0.2 + _fsz(out) / 960.0)


def f_red(out, in_):
    return _with(lambda e: e.reduce_sum(out=out, in_=in_, axis=mybir.AxisListType.X), 0.2 + _fsz(in_) / 960.0)


def f_cp(out, in_):
    return _with(lambda e: e.tensor_copy(out=out, in_=in_), 0.2 + _fsz(out) / 960.0)


def f_rcp(out, in_):
    return _with(lambda e: e.reciprocal(out=out, in_=in_), 0.2 + _fsz(out) / 160.0)


def f_ms(out, val):
    return _with(lambda e: e.memset(out, val), 0.2 + _fsz(out) / 960.0)


def f_dma(out, in_):
    return _with(lambda e: e.dma_start(out=out, in_=in_), 2.0 + 128 * _fsz(out) * 3.0 / 200e3)


PIECES = (["xk0", "xk1", "xv0", "xv1", "b", "a", "q", "k", "v", "g", "o0", "o1",
           "xq0", "xq1", "xo0", "xo1"]
          + [n for jp in range(4) for n in ("u%d" % (2 * jp), "u%d" % (2 * jp + 1),
                                           "d%d" % (2 * jp), "d%d" % (2 * jp + 1))])
PID = {n: i for i, n in enumerate(PIECES)}
TILE_SEQ = PIECES[4:]


def build_program(NT):
    S = NT * T
    nc = bass.Bass("TRN2", target_bir_lowering=False)

    def din(name, shape, dt=F32):
        return nc.dram_tensor(name, shape, dt, kind="ExternalInput").ap()

    xT = din("xT", [D, S])
    memT = din("memT", [D, NMEM])
    w_in = din("w_in", [D, 3072])
    w_out = din("w_out", [D, D])
    w_xq = din("w_xq", [D, D])
    w_xk = din("w_xk", [D, D])
    w_xv = din("w_xv", [D, D])
    w_xo = din("w_xo", [D, D])
    w_up = din("w_up", [D, 4096])
    w_down = din("w_down", [4096, D])
    vec_d = din("vec", [128, NV])
    gnt_d = din("gnt", [128, 2, 512])
    qdt_d = din("qdt", [128, 4, 128])
    mask_d = din("maskt", [128, 4, 128])
    cos_d = din("cost", [128, S])
    sin_d = din("sint", [128, S])
    idn_d = din("idn", [128, 128])
    prm_d = din("prm", [128, 128])
    one_d = din("one", [128, 128])
    outT = nc.dram_tensor("outT", [D, S], F32, kind="ExternalOutput").ap()
    wscr = nc.dram_tensor("wscr", [len(PIECES), 128, 4096], BF16).ap()

    def wsrc(name):
        def cols(w, c0):
            return w[:, c0:c0 + 512].rearrange("(c p) n -> p c n", p=128)
        if name in ("b", "a", "q", "k", "v", "g"):
            c0 = {"a": 0, "b": 512, "q": 1024, "k": 1536, "v": 2048, "g": 2560}[name]
            return cols(w_in, c0)
        if name[0] == "o":
            return cols(w_out, 512 * int(name[1:]))
        if name[:2] == "xq":
            return cols(w_xq, 512 * int(name[2:]))
        if name[:2] == "xk":
            return cols(w_xk, 512 * int(name[2:]))
        if name[:2] == "xv":
            return cols(w_xv, 512 * int(name[2:]))
        if name[:2] == "xo":
            return cols(w_xo, 512 * int(name[2:]))
        if name[0] == "u":
            return cols(w_up, 512 * int(name[1:]))
        if name[0] == "d":
            r0 = 512 * int(name[1:])
            return w_down[r0:r0 + 512, :].rearrange("(c p) n -> p c n", p=128)
        raise KeyError(name)

    with ExitStack() as es:
        def sb(name, shape, dt):
            return es.enter_context(nc.sbuf_tensor(name, shape, dt))

        RING = [sb("ring%d" % i, [128, 8, 512], BF16) for i in range(4)]
        RA = sb("RA", [128, 8, 512], F32)
        XB = sb("XB", [128, 8, 512], BF16)
        RB = sb("RB", [128, 8, 512], BF16)
        CATS = [sb("CAT%d" % i, [128, 8, 512], BF16) for i in range(2)]
        H = sb("H", [128, 8, 512], BF16)
        ZB = [sb("ZB%d" % i, [128, 512], BF16) for i in range(4)]
        ZSQ = [sb("ZSQ%d" % i, [128, 512], BF16) for i in range(4)]
        U = sb("U", [128, 4, 542], F32)
        SGB = sb("SGB", [128, 4, 512], F32)
        QBT = [sb("QBT%d" % i, [128, 512], BF16) for i in range(2)]
        T1 = [sb("T1_%d" % i, [128, 512], F32) for i in range(2)]
        T2 = [sb("T2_%d" % i, [128, 512], F32) for i in range(2)]
        QP = sb("QP", [128, 4, 512], BF16)
        QD = sb("QD", [128, 4, 512], BF16)
        KP = sb("KP", [128, 4, 512], BF16)
        KD = [sb("KD%d" % i, [128, 4, 128], BF16) for i in range(2)]
        V = sb("V", [128, 4, 512], BF16)
        SG = sb("SG", [128, 4, 512], BF16)
        CF = sb("CF", [128, 4, 512], F32)
        STATS = {"b": [sb("%s_b" % n, [128, 512], F32) for n in ("MEAN", "MSQ")],
                 "f": [T1[0], T2[0]]}
        SKEY = {"b": (("MEAN", "b"), ("MSQ", "b")), "f": (("T1", 0), ("T2", 0))}
        for w in ("f", "b"):
            STATS[w].append(STATS[w][1])
        PT = [sb("PT%d" % i, [128, 128], BF16) for i in range(4)]
        ST = sb("ST", [128, 4, 128], F32)
        STB = sb("STB", [128, 4, 128], BF16)
        YN = [sb("YN0", [128, 512], F32)] * 2
        RT = [sb("RT%d" % i, [128, 512], BF16) for i in range(2)]
        GNS = [sb("GNS%d" % i, [128, 32], F32) for i in range(2)]
        PX = [sb("PX%d" % i, [128, 2, 512], BF16) for i in range(2)]
        RCP = [sb("RCP0", [128, 512], F32)] * 2
        RL = [sb("RL0", [128, 512], F32)]
        COS = sb("COS", [128, 512], F32)
        SIN = sb("SIN", [128, 512], F32)
        QDT = sb("QDT", [128, 4, 128], F32)
        MASKT = sb("MASKT", [128, 4, 128], F32)
        IDN = sb("IDN", [128, 128], BF16)
        PRM = sb("PRM", [128, 128], BF16)
        ONE = sb("ONE", [128, 128], BF16)
        VEC = sb("VEC", [128, NV], F32)
        GNT = sb("GNT", [128, 2, 512], F32)
        KXT = sb("KXT", [128, 8, 256], BF16)
        VX = sb("VX", [128, 2, 1024], BF16)

        PSF = [es.enter_context(nc.psum_tensor("psf%d" % i, [128, 512], F32)) for i in range(NPSF)]
        PSB = [es.enter_context(nc.psum_tensor("psb%d" % i, [128, 1024], BF16)) for i in range(8 - NPSF)]
        if len(PSB) == 1:
            PSB = [PSB[0], PSB[0]]

        sems = {e: es.enter_context(nc.semaphore("s_" + e)) for e in ("pe", "act", "dve", "pool")}
        CKEYS = ["c_idn", "c_prm", "c_one", "c_vec", "c_gnt", "c_qdt", "c_mask"]
        K_IDN, K_PRM, K_ONE, K_VEC, K_GNT, K_QDT, K_MASK = CKEYS
        dkeys = ([("ring", i) for i in range(4)] + [("wb", i) for i in range(len(PIECES))]
                 + CKEYS + ["mem", "xb", "cos", "sin"] + [("ra", c) for c in range(8)] + [("out", c) for c in range(8)])
        dsems = {k: es.enter_context(nc.semaphore("d%d" % i)) for i, k in enumerate(dkeys)}
        block = es.enter_context(nc.Block())

        def vcol(c):
            return VEC[:, c:c + 1]
        C_CONVB, C_CG, C_CB = 0, 4, 8
        C_LN = {1: (12, 20), 2: (28, 36), 3: (44, 52)}
        C_KDC, C_EPS, C_CW = 60, 64, 65

        def bcast(t, off, n):
            return bass.AP(t, off, [[32, 128], [1, 4], [0, n]])

        class Banks:
            def __init__(self):
                self.i = 0
                self.held = set()

            def next(self):
                while True:
                    b = self.i % NPSF
                    self.i += 1
                    if b not in self.held:
                        return PSF[b], ("psf", b)

            def hold(self):
                t, k = self.next()
                self.held.add(k[1])
                return t, k

            def unhold(self, k):
                self.held.discard(k[1])

        class RecWS:
            def __init__(self):
                self.seq = []

            def start(self):
                pass

            def get(self, name):
                self.seq.append(name)
                return 0

            def release(self, slot):
                pass

        class WStream:
            def __init__(self, P, seq):
                self.P = P
                self.seq = seq
                self.n_issued = 0
                self.n_got = 0
                self.free = [0, 1, 2, 3]
                self.where = {}

            def _issue(self):
                if self.n_issued >= len(self.seq) or not self.free:
                    return
                i = self.n_issued
                slot = self.free.pop(0)
                pid = PID[self.seq[i]]
                self.P.add("sp", f_dma(RING[slot][:], wscr[pid].rearrange("p (c n) -> p c n", c=8)),
                           reads=[("wb", pid)], writes=[("ring", slot)], dma=("ring", slot))
                self.where[i] = slot
                self.n_issued += 1

            def start(self):
                for _ in range(4):
                    self._issue()

            def get(self, name):
                i = self.n_got
                assert self.seq[i] == name, (self.seq[i], name)
                assert i in self.where, "ring exhausted"
                self.n_got += 1
                return self.where.pop(i)

            def release(self, slot):
                self.free.append(slot)
                self._issue()

        def emit_all(P, WS, picks, replay):
            PS = Banks()
            ri = {}

            def nxt(name, n):
                v = ri.get(name, 0)
                ri[name] = v + 1
                return v % n

            def load_xb(t):
                P.add("pool", f_dma(XB[:], xT[:, t * T:(t + 1) * T].rearrange("(c p) s -> p c s", p=128)),
                      writes=[("XB", c) for c in range(8)], dma="xb")

            def load_ra(t, c):
                P.add("sp", f_dma(RA[:, c, :], xT[c * 128:(c + 1) * 128, t * T:(t + 1) * T]),
                      writes=[("RA", c)], dma=("ra", c))

            def load_cs(t):
                P.add("pool", f_dma(COS[:], cos_d[:, t * T:(t + 1) * T]), writes=["COS"], dma="cos")
                P.add("pool", f_dma(SIN[:], sin_d[:, t * T:(t + 1) * T]), writes=["SIN"], dma="sin")

            for dst, src, key in ((IDN, idn_d, K_IDN), (PRM, prm_d, K_PRM), (ONE, one_d, K_ONE), (VEC, vec_d, K_VEC),
                                  (GNT, gnt_d, K_GNT), (QDT, qdt_d, K_QDT), (MASKT, mask_d, K_MASK)):
                P.add("pool", f_dma(dst[:], src), writes=[key], dma=key)
            MB = H[:, :, 0:256]
            P.add("pool", f_dma(MB, memT.rearrange("(c p) m -> p c m", p=128)), writes=["MB"], dma="mem")
            load_xb(0)
            load_cs(0)
            for c in range(8):
                load_ra(0, c)
            for name in PIECES:
                pid = PID[name]
                src = wsrc(name)
                c = src.shape[1]
                P.add("pool", f_dma(wscr[pid].rearrange("p (c n) -> p c n", c=c), src),
                      writes=[("wb", pid)], dma=("wb", pid))
            WS.start()
            P.add("dve", f_ms(U[:, :, 0:30], 0.0), writes=[("U", c) for c in range(4)])
            P.add("dve", f_ms(ST[:], 0.0), writes=["ST"])
            P.add("dve", f_ms(STB[:], 0.0), writes=["STB"])

            for nm in ("xk0", "xk1"):
                s_ = WS.get(nm)
                for o4 in range(4):
                    oc = 4 * int(nm[2]) + o4
                    bt, bk = PS.next()
                    for kc in range(8):
                        P.add("pe", f_mm(bt[:, 0:256], RING[s_][:, kc, o4 * 128:(o4 + 1) * 128], H[:, kc, 0:256],
                                         kc == 0, kc == 7),
                              reads=[("ring", s_), "MB"], writes=[bk])
                    P.add("act", f_act(KXT[:, oc, :], bt[:, 0:256], AF.Copy), writes=[bk, "KXT"])
                WS.release(s_)
            for half, nm in enumerate(("xv0", "xv1")):
                s_ = WS.get(nm)
                for mc in range(2):
                    bt, bk = PS.next()
                    for kc in range(8):
                        P.add("pe", f_mm(bt[:], H[:, kc, mc * 128:(mc + 1) * 128], RING[s_][:, kc, :], kc == 0, kc == 7),
                              reads=[("ring", s_), "MB"], writes=[bk])
                    P.add("act", f_act(VX[:, mc, half * 512:(half + 1) * 512], bt[:], AF.Copy), writes=[bk, "VX"])
                WS.release(s_)
            P.add("dve", f_ms(H[:, 0, 0:2], 0.0), writes=["MB"] + [("H", c) for c in range(8)])

            pend_stats = []

            def flush_stats(keep=0):
                while len(pend_stats) > keep:
                    pend_stats.pop(0)()

            def ln_stat_chunk(src, src_key, sum_b, sq_b, first, last, defer=0):
                i = nxt("z", 4)
                P.add("act", f_act(ZB[i][:], src, AF.Copy), reads=[src_key], writes=[("ZB", i)])
                P.add("act", f_act(ZSQ[i][:], src, AF.Square), reads=[src_key], writes=[("ZSQ", i)])

                def mms():
                    P.add("pe", f_mm(sum_b[0][:], ONE[:], ZB[i][:], first, last), reads=[("ZB", i), K_ONE],
                          writes=[sum_b[1]])
                    P.add("pe", f_mm(sq_b[0][:], ONE[:], ZSQ[i][:], first, last), reads=[("ZSQ", i), K_ONE],
                          writes=[sq_b[1]])
                pend_stats.append(mms)
                flush_stats(defer)

            def ln_post(w, sum_b, sq_b, dd):
                MEAN, MSQ, RSTD = STATS[w]
                km, kq = SKEY[w]
                kr = kq
                inv = 1.0 / dd
                P.add("act", f_act(MEAN[:], sum_b[0][:], AF.Copy, scale=inv), writes=[sum_b[1], km])
                P.add("dve", f_tt(MSQ[:], MEAN[:], MEAN[:], ALU.mult), reads=[km], writes=[kq])
                P.add("dve", f_stt(MSQ[:], sq_b[0][:], inv, MSQ[:], ALU.mult, ALU.subtract),
                      writes=[sq_b[1], kq])
                P.add("act", f_act(RSTD[:], MSQ[:], AF.Ln, bias=vcol(C_EPS), scale=1.0),
                      reads=[kq, K_VEC], writes=[kr])
                P.add("act", f_act(RSTD[:], RSTD[:], AF.Exp, scale=-0.5), writes=[kr])
                PS.unhold(sum_b[1])
                PS.unhold(sq_b[1])

            def ln_norm_chunk(which, oc, make_rb):
                MEAN, MSQ, RSTD = STATS["b"]
                cg, cb = C_LN[which]
                e1 = "pool" if (LNMODE == 1 and oc % 2 == 1) else "dve"
                e2 = "dve" if (LNMODE == 2 and oc % 2 == 1) else "pool"
                P.add(e1, f_tt(RA[:, oc, :], RA[:, oc, :], MEAN[:], ALU.subtract),
                      reads=[("MEAN", "b")], writes=[("RA", oc)])
                P.add(e2, f_tt(RA[:, oc, :], RA[:, oc, :], RSTD[:], ALU.mult),
                      reads=[("MSQ", "b")], writes=[("RA", oc)])
                if make_rb:
                    P.add("act", f_act(RB[:, oc, :], RA[:, oc, :], AF.Identity, bias=vcol(cb + oc), scale=vcol(cg + oc)),
                          reads=[K_VEC, ("RA", oc)], writes=[("RB", oc)])
                P.add("act", f_act(RA[:, oc, :], RA[:, oc, :], AF.Identity, bias=vcol(cb + oc), scale=vcol(cg + oc)),
                      reads=[K_VEC], writes=[("RA", oc)])

            def residual_ln(which, pieces, rhs_buf, rhs_key, make_rb):
                s_ = None
                for oc in range(8):
                    if oc % 4 == 0:
                        s_ = WS.get(pieces[oc // 4])
                    bt, bk = PS.next()
                    for kc in range(8):
                        P.add("pe", f_mm(bt[:], RING[s_][:, kc, (oc % 4) * 128:(oc % 4 + 1) * 128], rhs_buf[:, kc, :],
                                         kc == 0, kc == 7),
                              reads=[("ring", s_), (rhs_key, kc)], writes=[bk])
                    P.add("dve", f_stt(RA[:, oc, :], RA[:, oc, :], ALPHA, bt[:], ALU.mult, ALU.add),
                          writes=[bk, ("RA", oc)])
                    P.add("act", f_act(H[:, oc, :], RA[:, oc, :], AF.Copy), reads=[("RA", oc)], writes=[("H", oc)])
                    P.add("act", f_act(RB[:, oc, :], RA[:, oc, :], AF.Square), reads=[("RA", oc)], writes=[("RB", oc)])
                    if oc % 4 == 3:
                        WS.release(s_)
                    yield
                sum_b = PS.hold()
                sq_b = PS.hold()
                for oc in range(8):
                    P.add("pe", f_mm(sum_b[0][:], ONE[:], H[:, oc, :], oc == 0, oc == 7), reads=[("H", oc), K_ONE],
                          writes=[sum_b[1]])
                for oc in range(8):
                    P.add("pe", f_mm(sq_b[0][:], ONE[:], RB[:, oc, :], oc == 0, oc == 7), reads=[("RB", oc), K_ONE],
                          writes=[sq_b[1]])
                ln_post("b", sum_b, sq_b, float(D))
                for oc in range(8):
                    ln_norm_chunk(which, oc, make_rb)
                    if oc % 2 == 1:
                        yield

            convq = []
            cat_free = [-1]

            def front(t):
                P.tile["f"] = t
                CAT = CATS[t % 2]
                ck = "CAT%d" % (t % 2)
                P.ph["f"] = "glu"
                s_ = WS.get("b")
                for cc in range(4):
                    b1, k1 = PS.next()
                    for kc in range(8):
                        P.add("pe", f_mm(b1[:], RING[s_][:, kc, cc * 128:(cc + 1) * 128], XB[:, kc, :], kc == 0, kc == 7),
                              reads=[("ring", s_), ("XB", kc)], writes=[k1])
                    P.add("act", f_act(SGB[:, cc, :], b1[:], AF.Sigmoid), writes=[k1, ("SGB", cc)])
                    yield
                WS.release(s_)
                s_ = WS.get("a")
                for cc in range(4):
                    b2, k2 = PS.next()
                    for kc in range(8):
                        P.add("pe", f_mm(b2[:], RING[s_][:, kc, cc * 128:(cc + 1) * 128], XB[:, kc, :], kc == 0, kc == 7),
                              reads=[("ring", s_), ("XB", kc)], writes=[k2])
                    P.add("dve", f_tt(U[:, cc, 30:542], b2[:], SGB[:, cc, :], ALU.mult),
                          reads=[("SGB", cc)], writes=[k2, ("U", cc)])
                    yield
                WS.release(s_)

                for cc in range(4):
                    convq.append(lambda cc=cc: P.add(
                        "pool", f_ts(CF[:, cc, :], U[:, cc, 0:512], vcol(C_CW + cc * 31), vcol(C_CONVB + cc),
                                     ALU.mult, ALU.add),
                        reads=[("U", cc), K_VEC], writes=[("CF", cc)]))
                    for k in range(1, 31):
                        convq.append(lambda cc=cc, k=k: P.add(
                            "dve", _with(f_stt(CF[:, cc, :], U[:, cc, k:k + 512], vcol(C_CW + cc * 31 + k), CF[:, cc, :],
                                               ALU.mult, ALU.add), 0.73 * CONVSCALE),
                            reads=[("U", cc)], writes=[("CF", cc)]))
                    convq.append(lambda cc=cc: P.add("pool", f_cp(U[:, cc, 0:30], U[:, cc, 512:542]),
                                                     writes=[("U", cc)]))

                def drain(n):
                    cur = P.cur
                    for _ in range(min(n, len(convq))):
                        convq.pop(0)()
                    P.cur = cur

                P.ph["f"] = "qk"
                pend_rot = []
                for nm in ("q", "k"):
                    s_ = WS.get(nm)
                    sc = 128.0 ** -0.5 if nm == "q" else 1.0
                    dest, dkey = (QP, "QP") if nm == "q" else (KP, "KP")
                    for h in range(4):
                        b1, k1 = PS.hold()
                        for kc in range(8):
                            P.add("pe", f_mm(b1[:], RING[s_][:, kc, h * 128:(h + 1) * 128], XB[:, kc, :], kc == 0, kc == 7),
                                  reads=[("ring", s_), ("XB", kc)], writes=[k1])
                        i = nxt("rot", 2)
                        P.add("act", f_act(QBT[i][:], b1[:], AF.Copy, scale=sc), writes=[k1, ("QBT", i)])

                        def rot(b1=b1, k1=k1, i=i, sc=sc, dest=dest, dkey=dkey, h=h, nm=nm):
                            b2, k2 = PS.next()
                            P.add("pe", f_mm(b2[:], PRM[:], QBT[i][:], True, True), reads=[("QBT", i), K_PRM], writes=[k2])
                            P.add("dve", f_stt(T1[i][:], b1[:], sc, COS[:], ALU.mult, ALU.mult),
                                  reads=["COS"], writes=[k1, ("T1", i)])
                            PS.unhold(k1)
                            P.add("dve", f_tt(T2[i][:], b2[:], SIN[:], ALU.mult), reads=["SIN"], writes=[k2, ("T2", i)])
                            P.add("pool", f_tt(dest[:, h, :], T1[i][:], T2[i][:], ALU.add),
                                  reads=[("T1", i), ("T2", i)], writes=[(dkey, h)])
                            if nm == "q":
                                for j in range(4):
                                    P.add("pool", f_tt(QD[:, h, j * 128:(j + 1) * 128], QP[:, h, j * 128:(j + 1) * 128],
                                                       QDT[:, h, :], ALU.mult),
                                          reads=[("QP", h), K_QDT], writes=[("QD", h)])
                        pend_rot.append(rot)
                        while len(pend_rot) > 1:
                            pend_rot.pop(0)()
                        drain(FD_QK)
                        yield
                    WS.release(s_)
                while pend_rot:
                    pend_rot.pop(0)()
                yield
                if t + 1 < NT:
                    load_cs(t + 1)

                P.ph["f"] = "vg"
                for nm in ("v", "g"):
                    s_ = WS.get(nm)
                    for j in range(4):
                        b1, k1 = PS.next()
                        for kc in range(8):
                            P.add("pe", f_mm(b1[:], XB[:, kc, j * 128:(j + 1) * 128], RING[s_][:, kc, :], kc == 0, kc == 7),
                                  reads=[("ring", s_), ("XB", kc)], writes=[k1])
                        if nm == "v":
                            P.add("act", f_act(V[:, j, :], b1[:], AF.Copy), writes=[k1, ("V", j)])
                        else:
                            P.add("act", f_act(SG[:, j, :], b1[:], AF.Silu), writes=[k1, ("SG", j)])
                        drain(FD_VG)
                        yield
                    WS.release(s_)
                if t + 1 < NT:
                    load_xb(t + 1)

                P.ph["f"] = "ret"
                pend_tail = []
                for j in range(4):
                    js = slice(j * 128, (j + 1) * 128)
                    bs, ks = PS.hold()
                    for h in range(4):
                        P.add("pe", f_mm(bs[:, h * 128:(h + 1) * 128], KP[:, h, js], QP[:, h, js], True, True),
                              reads=[("KP", h), ("QP", h)], writes=[ks])
                    for h in range(4):
                        P.add("pe", f_tr(PSB[0][:, h * 128:(h + 1) * 128], KP[:, h, js], IDN[:]),
                              reads=[("KP", h), K_IDN], writes=[("psb", 0)])
                    kdi = nxt("kd", 2)
                    for h in range(4):
                        P.add("act", f_act(KD[kdi][:, h, :], PSB[0][:, h * 128:(h + 1) * 128], AF.Copy,
                                           scale=vcol(C_KDC + h)),
                              reads=[K_VEC], writes=[("psb", 0), ("KD", kdi)])
                    drain(FD_RET)
                    yield
                    by, ky = PS.hold()
                    for h in range(4):
                        pi = nxt("pt", 4)
                        P.add("dve", f_tt(PT[pi][:], bs[:, h * 128:(h + 1) * 128], MASKT[:, h, :], ALU.mult),
                              reads=[K_MASK], writes=[ks, ("PT", pi)])
                        P.add("pe", f_mm(by[:, h * 128:(h + 1) * 128], PT[pi][:], V[:, j, h * 128:(h + 1) * 128],
                                         True, False),
                              reads=[("PT", pi), ("V", j)], writes=[ky])
                        P.add("pe", f_mm(by[:, h * 128:(h + 1) * 128], QD[:, h, js], STB[:, h, :], False, True),
                              reads=[("QD", h), "STB"], writes=[ky])
                    PS.unhold(ks)
                    drain(FD_RET)
                    yield
                    bu, ku = PS.next()
                    for h in range(4):
                        P.add("pe", f_mm(bu[:, h * 128:(h + 1) * 128], KD[kdi][:, h, :], V[:, j, h * 128:(h + 1) * 128],
                                         True, True),
                              reads=[("KD", kdi), ("V", j)], writes=[ku])
                    for h in range(4):
                        P.add("dve", f_stt(ST[:, h, :], ST[:, h, :], GAM[h] ** 128, bu[:, h * 128:(h + 1) * 128],
                                           ALU.mult, ALU.add),
                              writes=[ku, "ST"])
                    P.add("act", f_act(STB[:], ST[:], AF.Copy), reads=["ST"], writes=["STB"])
                    while pend_tail:
                        pend_tail.pop(0)()
                    drain(FD_RET)
                    yield
                    yi = nxt("yn", 2)
                    g = GNS[yi]
                    gk = ("GNS", yi)
                    by3 = by[:].rearrange("p (h e) -> p h e", h=4)
                    yn3 = YN[yi][:].rearrange("p (h e) -> p h e", h=4)
                    P.add("act", f_act(YN[yi][:], by[:], AF.Square), writes=[ky, ("YN", 0)])
                    P.add("dve", f_red(g[:, 0:4], by3), writes=[ky, gk])
                    P.add("dve", f_red(g[:, 4:8], yn3), reads=[("YN", 0)], writes=[gk])
                    P.add("dve", f_ts1(g[:, 8:12], g[:, 0:4], 1.0 / 128, ALU.mult), writes=[gk])
                    P.add("dve", f_tt(g[:, 12:16], g[:, 8:12], g[:, 8:12], ALU.mult), writes=[gk])
                    P.add("dve", f_stt(g[:, 16:20], g[:, 4:8], 1.0 / 128, g[:, 12:16], ALU.mult, ALU.subtract),
                          writes=[gk])
                    P.add("act", f_act(g[:, 20:24], g[:, 16:20], AF.Sqrt, bias=vcol(C_EPS), scale=1.0),
                          reads=[K_VEC], writes=[gk])
                    P.add("dve", f_rcp(g[:, 20:24], g[:, 20:24]), writes=[gk])
                    P.add("dve", f_tt(yn3, by3, bcast(g, 8, 128), ALU.subtract), reads=[gk], writes=[ky, ("YN", 0)])
                    P.add("dve", f_tt(yn3, yn3, bcast(g, 20, 128), ALU.mult), reads=[gk], writes=[("YN", 0)])
                    P.add("pool", f_tt(YN[yi][:], YN[yi][:], GNT[:, 0, :], ALU.mult), reads=[K_GNT], writes=[("YN", 0)])
                    P.add("pool", f_tt(YN[yi][:], YN[yi][:], GNT[:, 1, :], ALU.add), reads=[K_GNT], writes=[("YN", 0)])
                    P.add("pool", f_tt(RT[yi][:], YN[yi][:], SG[:, j, :], ALU.mult), reads=[("YN", 0), ("SG", j)],
                          writes=[("RT", yi)])
                    def tail(yi=yi, js=js):
                        for h in range(4):
                            P.add("pe", f_tr(PSB[1][:, h * 128:(h + 1) * 128], RT[yi][:, h * 128:(h + 1) * 128], IDN[:]),
                                  reads=[("RT", yi), K_IDN], writes=[("psb", PSB1K)])
                        P.add("act", f_act(CAT[:, 4:8, js], PSB[1][:, 0:512].rearrange("p (h n) -> p h n", h=4), AF.Copy),
                              writes=[("psb", PSB1K)] + [(ck, 4 + h) for h in range(4)])
                    pend_tail.append(tail)
                    PS.unhold(ky)
                    drain(FD_RET)
                    yield
                while pend_tail:
                    pend_tail.pop(0)()
                drain(len(convq))
                yield
                P.ph["f"] = "convln"
                sum_b = PS.hold()
                sq_b = PS.hold()
                for cc in range(4):
                    ln_stat_chunk(CF[:, cc, :], ("CF", cc), sum_b, sq_b, cc == 0, cc == 3)
                ln_post("f", sum_b, sq_b, 512.0)
                MEAN, MSQ, RSTD = STATS["f"]
                for cc in range(4):
                    P.add("dve", f_tt(CF[:, cc, :], CF[:, cc, :], MEAN[:], ALU.subtract), reads=[SKEY["f"][0]],
                          writes=[("CF", cc)])
                    P.add("pool", f_tt(CF[:, cc, :], CF[:, cc, :], RSTD[:], ALU.mult), reads=[SKEY["f"][1]],
                          writes=[("CF", cc)])
                    P.add("act", f_act(CAT[:, cc, :], CF[:, cc, :], AF.Silu, bias=vcol(C_CB + cc), scale=vcol(C_CG + cc)),
                          reads=[("CF", cc), K_VEC], writes=[(ck, cc)])
                yield

            def back(t):
                P.tile["b"] = t
                CAT = CATS[t % 2]
                ck = "CAT%d" % (t % 2)
                P.ph["b"] = "wout"
                yield from residual_ln(1, ("o0", "o1"), CAT, ck, True)
                P.ph["b"] = "xattn"
                s_ = None
                for oc in range(8):
                    if oc % 4 == 0:
                        s_ = WS.get("xq%d" % (oc // 4))
                    b1, k1 = PS.next()
                    for kc in range(8):
                        P.add("pe", f_mm(b1[:], RING[s_][:, kc, (oc % 4) * 128:(oc % 4 + 1) * 128], RB[:, kc, :],
                                         kc == 0, kc == 7),
                              reads=[("ring", s_), ("RB", kc)], writes=[k1])
                    P.add("act", f_act(CAT[:, oc, :], b1[:], AF.Copy), writes=[k1, (ck, oc)])
                    if oc % 4 == 3:
                        WS.release(s_)
                    yield
                for h in range(4):
                    pi = nxt("px", 2)
                    for mc in range(2):
                        b1, k1 = PS.next()
                        for dc in range(2):
                            P.add("pe", f_mm(b1[:], KXT[:, 2 * h + dc, mc * 128:(mc + 1) * 128], CAT[:, 2 * h + dc, :],
                                             dc == 0, dc == 1),
                                  reads=["KXT", (ck, 2 * h + dc)], writes=[k1])
                        P.add("act", f_act(PX[pi][:, mc, :], b1[:], AF.Exp, scale=1.0 / 16.0), writes=[k1, ("PX", pi)])
                    yield
                    b1, k1 = PS.next()
                    for mc in range(2):
                        P.add("pe", f_mm(b1[:], ONE[:], PX[pi][:, mc, :], mc == 0, mc == 1),
                              reads=[("PX", pi), K_ONE], writes=[k1])
                    P.add("act", f_act(RCP[pi][:], b1[:], AF.Ln), writes=[k1, ("RCP", 0)])
                    P.add("act", f_act(RCP[pi][:], RCP[pi][:], AF.Exp, scale=-1.0), writes=[("RCP", 0)])
                    for dc in range(2):
                        b2, k2 = PS.next()
                        for mc in range(2):
                            P.add("pe", f_mm(b2[:], VX[:, mc, (2 * h + dc) * 128:(2 * h + dc + 1) * 128],
                                             PX[pi][:, mc, :], mc == 0, mc == 1),
                                  reads=["VX", ("PX", pi)], writes=[k2])
                        P.add("dve", f_tt(CAT[:, 2 * h + dc, :], b2[:], RCP[pi][:], ALU.mult),
                              reads=[("RCP", 0)], writes=[k2, (ck, 2 * h + dc)])
                    yield
                P.ph["b"] = "wxo"
                yield from residual_ln(2, ("xo0", "xo1"), CAT, ck, True)
                cat_free[0] = t
                P.ph["b"] = "mlp"

                for jp in range(4):
                    if jp == 3:
                        sum_b = PS.hold()
                        sq_b = PS.hold()
                    s_ = None
                    for hc in range(8):
                        if hc % 4 == 0:
                            s_ = WS.get("u%d" % (2 * jp + hc // 4))
                        b1, k1 = PS.next()
                        for kc in range(8):
                            P.add("pe", f_mm(b1[:], RING[s_][:, kc, (hc % 4) * 128:(hc % 4 + 1) * 128], RB[:, kc, :],
                                             kc == 0, kc == 7),
                                  reads=[("ring", s_), ("RB", kc)], writes=[k1])
                        i = 0
                        P.add("act", f_act(RL[i][:], b1[:], AF.Relu), writes=[k1, ("RL", i)])
                        P.add("act", f_act(H[:, hc, :], RL[i][:], AF.Square), reads=[("RL", i)], writes=[("H", hc)])
                        if hc % 4 == 3:
                            WS.release(s_)
                        yield
                    sd = [WS.get("d%d" % (2 * jp)), WS.get("d%d" % (2 * jp + 1))]
                    for oc in range(8):
                        b1, k1 = PS.next()
                        for kc8 in range(8):
                            s_ = sd[kc8 // 4]
                            kk = kc8 % 4
                            P.add("pe", f_mm(b1[:], RING[s_][:, kk * 2 + oc // 4, (oc % 4) * 128:(oc % 4 + 1) * 128],
                                             H[:, kc8, :], kc8 == 0, kc8 == 7),
                                  reads=[("ring", s_), ("H", kc8)], writes=[k1])
                        if jp == 0:
                            P.add("dve", f_stt(RA[:, oc, :], RA[:, oc, :], ALPHA, b1[:], ALU.mult, ALU.add),
                                  writes=[k1, ("RA", oc)])
                        else:
                            P.add("dve", f_tt(RA[:, oc, :], RA[:, oc, :], b1[:], ALU.add), writes=[k1, ("RA", oc)])
                        if jp == 3:
                            ln_stat_chunk(RA[:, oc, :], ("RA", oc), sum_b, sq_b, oc == 0, oc == 7, defer=2)
                        yield
                    WS.release(sd[0])
                    WS.release(sd[1])
                flush_stats()
                ln_post("b", sum_b, sq_b, float(D))
                for oc in range(8):
                    ln_norm_chunk(3, oc, False)
                    P.add("sp", f_dma(outT[oc * 128:(oc + 1) * 128, t * T:(t + 1) * T], RA[:, oc, :]),
                          reads=[("RA", oc)], dma=("out", oc))
                    if oc >= 1 and t + 1 < NT:
                        load_ra(t + 1, oc - 1)
                    if oc % 2 == 1:
                        yield
                if t + 1 < NT:
                    load_ra(t + 1, 7)

            def chain_front():
                for t in range(NT):
                    while t >= 2 and (back_done[0] if NOAHEAD else cat_free[0]) < t - 2:
                        yield "blocked"
                    yield from front(t)
                    front_done[0] = t

            def chain_back():
                for t in range(NT):
                    while front_done[0] < t:
                        yield "blocked"
                    yield from back(t)
                    back_done[0] = t

            front_done = [-1]
            back_done = [-1]
            gf = chain_front()
            gb_ = chain_back()
            tf = tb = 0.0
            alive_f = alive_b = True
            blocked_f = blocked_b = False
            while alive_f or alive_b:
                can_b = alive_b and not blocked_b
                can_f = alive_f and not blocked_f
                if can_b and can_f:
                    pick_b = tb <= tf + BIAS
                elif can_b or can_f:
                    pick_b = can_b
                else:
                    blocked_f = blocked_b = False
                    continue
                if replay is not None:
                    pick_b = replay.pop(0)
                else:
                    picks.append(pick_b)
                P.mark = 0.0
                P.first = None
                P.cur = "b" if pick_b else "f"
                try:
                    r = next(gb_ if pick_b else gf)
                except StopIteration:
                    if pick_b:
                        alive_b = False
                    else:
                        alive_f = False
                    blocked_f = blocked_b = False
                    continue
                if r == "blocked":
                    if pick_b:
                        blocked_b = True
                    else:
                        blocked_f = True
                    continue
                blocked_f = blocked_b = False
                if pick_b:
                    tb = max(tb, P.mark)
                else:
                    tf = max(tf, P.mark)
            P.sim_total = max(P.free.values())

        FD_QK, FD_VG, FD_RET, BD = 6, 5, 2, 0
        BIAS = 5.0
        NOAHEAD = False
        CONVSCALE = 1.0
        rec = RecWS()
        picks = []
        emit_all(Prog(), rec, picks, None)
        P = Prog()
        P.dump = False
        emit_all(P, WStream(P, rec.seq), None, list(picks))
        P.emit(block, sems, dsems, {"sp": [("out", c) for c in range(8)]})
    return nc


def _constants(S):
    pos = np.arange(S, dtype=np.float32)
    inv = (np.float32(10000.0) ** (-np.linspace(0.0, 1.0, 64, dtype=np.float32))).astype(np.float32)
    ang = (pos[:, None] * inv[None, :]).astype(np.float32)
    cos = np.cos(ang).astype(np.float32).T
    sin = np.sin(ang).astype(np.float32).T
    cost = np.concatenate([cos, cos], axis=0)
    sint = np.concatenate([-sin, sin], axis=0)
    gam = np.array(GAM, dtype=np.float64)
    n = np.arange(128, dtype=np.float64)
    qdt = np.broadcast_to((gam[:, None] ** (n[None, :] + 1.0))[None], (128, 4, 128)).astype(np.float32)
    m = n[:, None]
    nn = n[None, :]
    maskt = np.stack([np.where(m <= nn, g ** (nn - m), 0.0) for g in gam], axis=1).astype(np.float32)
    kdc = np.stack([g ** (127.0 - n) for g in gam], axis=1).astype(np.float32)
    idn = np.eye(128, dtype=np.float32)
    prm = np.roll(idn, 64, axis=1).copy()
    one = np.ones((128, 128), dtype=np.float32)
    return (np.ascontiguousarray(cost), np.ascontiguousarray(sint), np.ascontiguousarray(qdt),
            np.ascontiguousarray(maskt), kdc, idn, prm, one)


def _fm(v, nchunk):
    return np.asarray(v, dtype=np.float32).reshape(nchunk, 128).T


def _prepare(inputs, S):
    cost, sint, qdt, maskt, kdc, idn, prm, one = _constants(S)
    vec = np.zeros((128, NV), dtype=np.float32)
    vec[:, 0:4] = _fm(inputs["conv_b"][0], 4)
    vec[:, 4:8] = _fm(inputs["conv_ln_g"][0], 4)
    vec[:, 8:12] = _fm(inputs["conv_ln_b"][0], 4)
    for i, nm in enumerate(("ln1_g", "ln1_b", "ln2_g", "ln2_b", "ln3_g", "ln3_b")):
        vec[:, 12 + 8 * i:20 + 8 * i] = _fm(inputs[nm][0], 8)
    vec[:, 60:64] = kdc
    vec[:, 64] = EPS
    cw = np.asarray(inputs["conv_w"][0], dtype=np.float32)
    vec[:, 65:65 + 124] = cw.T.reshape(4, 128, 31).transpose(1, 0, 2).reshape(128, 124)
    gnt = np.stack([np.broadcast_to(np.asarray(inputs["ret_gn_g"][0], np.float32)[None, :], (128, 512)),
                    np.broadcast_to(np.asarray(inputs["ret_gn_b"][0], np.float32)[None, :], (128, 512))], axis=1)
    shared = {
        "w_in": np.ascontiguousarray(inputs["w_in"][0], dtype=np.float32),
        "w_out": np.ascontiguousarray(inputs["w_out"][0], dtype=np.float32),
        "w_xq": np.ascontiguousarray(inputs["w_xq"][0], dtype=np.float32),
        "w_xk": np.ascontiguousarray(inputs["w_xk"][0], dtype=np.float32),
        "w_xv": np.ascontiguousarray(inputs["w_xv"][0], dtype=np.float32),
        "w_xo": np.ascontiguousarray(inputs["w_xo"][0], dtype=np.float32),
        "w_up": np.ascontiguousarray(inputs["w_up"][0], dtype=np.float32),
        "w_down": np.ascontiguousarray(inputs["w_down"][0], dtype=np.float32),
        "vec": vec, "gnt": np.ascontiguousarray(gnt), "qdt": qdt, "maskt": maskt,
        "cost": cost, "sint": sint, "idn": idn, "prm": prm, "one": one,
    }
    return shared


def run(inputs, S, n_cores):
    shared = _prepare(inputs, S)
    x = np.asarray(inputs["x"], dtype=np.float32)
    mem = np.asarray(inputs["mem"], dtype=np.float32)
    in_maps = []
    for b in range(n_cores):
        m = dict(shared)
        m["xT"] = np.ascontiguousarray(x[b, :S].T)
        m["memT"] = np.ascontiguousarray(mem[b].T)
        in_maps.append(m)
    nc = build_program(S // T)
    res = run_bass_kernel_spmd(nc, in_maps, core_ids=list(range(n_cores)))
    out = np.stack([np.ascontiguousarray(res.results[b]["outT"].T) for b in range(n_cores)], axis=0)
    return out.astype(np.float32)


def kernel(**inputs):
    return run(inputs, SEQ, NB)
```
